# Optimizing a Trainium2 kernel written in Bass

```python
import jax, jax.numpy as jnp
from jax import lax
import numpy as np

D_MODEL = 1024
BATCH = 4
SEQ = 4096
DEPTH = 4

D_MIX = D_MODEL
GLA_HEADS = 4
GLA_DK = 64
GLA_DV = 128
GLA_WIDTH = GLA_HEADS * GLA_DV
GLA_KEY_WIDTH = GLA_HEADS * GLA_DK
GLA_GATE_RANK = 16
GLA_GATE_TEMP = 16.0
GLA_CHUNK = 64
MLA_HEADS = 4
MLA_NOPE = 128
MLA_ROPE = 64
MLA_QK = MLA_NOPE + MLA_ROPE
MLA_DV = 128
MLA_WIDTH = MLA_HEADS * MLA_DV
MLA_Q_RANK = 256
MLA_KV_RANK = 128
ROPE_THETA = 10000.0
Q_BLOCK = 128
EPS = 1e-6

IN_SPLITS = (GLA_KEY_WIDTH, GLA_KEY_WIDTH, GLA_WIDTH, GLA_GATE_RANK, GLA_WIDTH,
             MLA_Q_RANK, MLA_KV_RANK, MLA_ROPE, MLA_WIDTH)
D_IN = sum(IN_SPLITS)

kernel_name = "hymba_gla_mla_hybrid_trunk"


def rmsnorm(t, g):
    t32 = t.astype(jnp.float32)
    y = t32 * lax.rsqrt(jnp.mean(t32 * t32, axis=-1, keepdims=True) + EPS)
    return (y * g.astype(jnp.float32)).astype(t.dtype)


def apply_rope(t, cos, sin):
    half = t.shape[-1] // 2
    t32 = t.astype(jnp.float32)
    t1, t2 = t32[..., :half], t32[..., half:]
    return jnp.concatenate([t1 * cos - t2 * sin, t1 * sin + t2 * cos], axis=-1).astype(t.dtype)


def gla_mix(q, k, v, log_a, norm_g):
    B, S = q.shape[0], q.shape[1]
    n_chunks = S // GLA_CHUNK

    def to_chunks(t):
        return t.astype(jnp.float32).reshape(B, n_chunks, GLA_CHUNK, GLA_HEADS, -1).transpose(0, 3, 1, 2, 4)

    qc = to_chunks(q) * (GLA_DK ** -0.5)
    kc, vc, lac = to_chunks(k), to_chunks(v), to_chunks(log_a)
    b = jnp.cumsum(lac, axis=3)
    b_last = b[:, :, :, -1:, :]
    q_dec = qc * jnp.exp(b)
    k_inv = kc * jnp.exp(-b)
    k_end = kc * jnp.exp(b_last - b)

    causal = jnp.tril(jnp.ones((GLA_CHUNK, GLA_CHUNK), dtype=bool))
    a_intra = jnp.where(causal, jnp.einsum('bhnik,bhnjk->bhnij', q_dec, k_inv), 0.0)
    o_intra = jnp.einsum('bhnij,bhnjv->bhniv', a_intra, vc)

    chunk_update = jnp.einsum('bhnjk,bhnjv->bhnkv', k_end, vc)
    chunk_decay = jnp.exp(b_last[:, :, :, 0, :])

    def step(state, inp):
        decay, upd = inp
        return decay[..., None] * state + upd, state

    state0 = jnp.zeros((B, GLA_HEADS, GLA_DK, GLA_DV), jnp.float32)
    _, state_prev = lax.scan(step, state0,
                             (jnp.moveaxis(chunk_decay, 2, 0), jnp.moveaxis(chunk_update, 2, 0)))
    state_prev = jnp.moveaxis(state_prev, 0, 2)
    o_inter = jnp.einsum('bhnik,bhnkv->bhniv', q_dec, state_prev)

    o = (o_intra + o_inter).transpose(0, 2, 3, 1, 4).reshape(B, S, GLA_HEADS, GLA_DV)
    o = rmsnorm(o.astype(q.dtype), norm_g)
    return o.reshape(B, S, GLA_WIDTH)


def block_causal_attention(q, k, v):
    B, S = q.shape[0], q.shape[1]
    n_blocks = S // Q_BLOCK
    scale = MLA_QK ** -0.5
    kt = k.transpose(0, 2, 1, 3)
    vt = v.transpose(0, 2, 1, 3)
    qb = q.transpose(0, 2, 1, 3).reshape(B, MLA_HEADS, n_blocks, Q_BLOCK, MLA_QK).transpose(2, 0, 1, 3, 4)
    key_idx = jnp.arange(S)

    def one_block(args):
        q_blk, i = args
        s = jnp.einsum('bhqd,bhkd->bhqk', q_blk, kt).astype(jnp.float32) * scale
        q_idx = i * Q_BLOCK + jnp.arange(Q_BLOCK)
        s = jnp.where(key_idx[None, :] <= q_idx[:, None], s, -jnp.inf)
        p = jax.nn.softmax(s, axis=-1).astype(v.dtype)
        return jnp.einsum('bhqk,bhkd->bhqd', p, vt)

    o = lax.map(one_block, (qb, jnp.arange(n_blocks)))
    return o.transpose(1, 0, 3, 2, 4).reshape(B, S, MLA_HEADS * MLA_DV)


def setup_inputs(seed: int = 0) -> dict:
    key = jax.random.key(seed)
    ks = jax.random.split(key, 16)
    f32 = jnp.float32

    def w(k, shape, fan_in):
        return jax.random.normal(k, shape, f32) * (fan_in ** -0.5)

    def gain(k, shape):
        return 1.0 + 0.02 * jax.random.normal(k, shape, f32)

    x = jax.random.normal(ks[0], (BATCH, SEQ, D_MODEL), f32)
    offsets = jax.random.randint(ks[1], (BATCH, 1), 0, 1024, dtype=jnp.int32)
    positions = offsets + jnp.arange(SEQ, dtype=jnp.int32)[None, :]
    return {
        "x": x,
        "positions": positions,
        "norm_g": gain(ks[2], (DEPTH, D_MODEL)),
        "w_in": w(ks[3], (DEPTH, D_MODEL, D_IN), D_MODEL),
        "w_gla_gate_up": w(ks[4], (DEPTH, GLA_GATE_RANK, GLA_KEY_WIDTH), GLA_GATE_RANK),
        "b_gla_gate": 0.01 * jax.random.normal(ks[5], (DEPTH, GLA_KEY_WIDTH), f32),
        "gla_norm_g": gain(ks[6], (DEPTH, GLA_DV)),
        "mla_q_norm_g": gain(ks[7], (DEPTH, MLA_Q_RANK)),
        "w_uq": w(ks[8], (DEPTH, MLA_Q_RANK, MLA_HEADS * MLA_QK), MLA_Q_RANK),
        "mla_kv_norm_g": gain(ks[9], (DEPTH, MLA_KV_RANK)),
        "w_ukv": w(ks[10], (DEPTH, MLA_KV_RANK, MLA_HEADS * (MLA_NOPE + MLA_DV)), MLA_KV_RANK),
        "q_head_g": gain(ks[11], (DEPTH, MLA_QK)),
        "k_head_g": gain(ks[12], (DEPTH, MLA_QK)),
        "w_out": w(ks[13], (DEPTH, D_MIX, D_MODEL), D_MIX),
    }


def reference(x, positions, norm_g, w_in, w_gla_gate_up, b_gla_gate, gla_norm_g,
              mla_q_norm_g, w_uq, mla_kv_norm_g, w_ukv, q_head_g, k_head_g, w_out):
    B, S = x.shape[0], x.shape[1]
    split_idx = [int(v) for v in np.cumsum(IN_SPLITS)[:-1]]

    inv_freq = ROPE_THETA ** (-jnp.arange(0, MLA_ROPE, 2, dtype=jnp.float32) / MLA_ROPE)
    ang = positions.astype(jnp.float32)[..., None] * inv_freq
    cos = jnp.cos(ang)[:, :, None, :]
    sin = jnp.sin(ang)[:, :, None, :]

    for l in range(DEPTH):
        h = rmsnorm(x, norm_g[l])
        z = h @ w_in[l]
        (g_q, g_k, g_v, g_lr, g_gate,
         c_q, c_kv, k_pe, m_gate) = jnp.split(z, split_idx, axis=-1)

        gate_logit = (g_lr @ w_gla_gate_up[l] + b_gla_gate[l]).astype(jnp.float32)
        log_a = jax.nn.log_sigmoid(gate_logit) / GLA_GATE_TEMP
        o_gla = gla_mix(g_q.reshape(B, S, GLA_HEADS, GLA_DK),
                        g_k.reshape(B, S, GLA_HEADS, GLA_DK),
                        g_v.reshape(B, S, GLA_HEADS, GLA_DV),
                        log_a.reshape(B, S, GLA_HEADS, GLA_DK),
                        gla_norm_g[l])
        o_gla = o_gla * jax.nn.silu(g_gate)

        q = (rmsnorm(c_q, mla_q_norm_g[l]) @ w_uq[l]).reshape(B, S, MLA_HEADS, MLA_QK)
        kv = (rmsnorm(c_kv, mla_kv_norm_g[l]) @ w_ukv[l]).reshape(B, S, MLA_HEADS, MLA_NOPE + MLA_DV)
        k_nope, v = kv[..., :MLA_NOPE], kv[..., MLA_NOPE:]
        k_rope = jnp.broadcast_to(k_pe[:, :, None, :], (B, S, MLA_HEADS, MLA_ROPE))
        k = jnp.concatenate([k_nope, k_rope], axis=-1)
        q = rmsnorm(q, q_head_g[l])
        k = rmsnorm(k, k_head_g[l])
        q = jnp.concatenate([q[..., :MLA_NOPE], apply_rope(q[..., MLA_NOPE:], cos, sin)], axis=-1)
        k = jnp.concatenate([k[..., :MLA_NOPE], apply_rope(k[..., MLA_NOPE:], cos, sin)], axis=-1)
        o_mla = block_causal_attention(q, k, v) * jax.nn.silu(m_gate)

        x = x + jnp.concatenate([o_gla, o_mla], axis=-1) @ w_out[l]
    return x
```

```python
import numpy as np
import concourse.bass as bass
import concourse.mybir as mybir
from concourse.bass_utils import run_bass_kernel_spmd

F32 = mybir.dt.float32
BF16 = mybir.dt.bfloat16
I32 = mybir.dt.int32
AF = mybir.ActivationFunctionType
ALU = mybir.AluOpType

D = 1024
DIN = 2512
DEPTH = 4
SEQ = 4096
BATCH = 4
EPS = 1e-6
C_GQ, C_GK, C_GV, C_GG, C_LR, C_CQ, C_CKV, C_KPE, C_MG = 0, 256, 512, 1024, 1536, 1552, 1808, 1936, 2000
TWO_PI = 6.283185307179586
CW1 = 6.28125
CW2 = TWO_PI - CW1
PI = 3.141592653589793
NVEC = 16
VARIANT = ""
SKIP_OLD_SELF = False
PIPE_DIST = 1


class Ev:
    __slots__ = ("key", "sem", "val")

    def __init__(self, key, sem, val):
        self.key, self.sem, self.val = key, sem, val


class Tok:
    __slots__ = ("name", "w", "rs")

    def __init__(self, name):
        self.name, self.w, self.rs = name, None, []


class Sched:
    def __init__(self, nc, n_dma_sems=14):
        self.nc = nc
        self.eng = {}
        for name, h in (("pe", nc.tensor), ("act", nc.scalar), ("dve", nc.vector),
                        ("pool", nc.gpsimd), ("sp", nc.sync)):
            self.eng[name] = dict(h=h, sem=nc.alloc_semaphore(name=f"s_{name}"), n=0, seen={}, pending=[])
        self.dsems = [nc.alloc_semaphore(name=f"s_dma{i}") for i in range(n_dma_sems)]
        self.dcount = [0] * n_dma_sems
        self.dnext = 0
        self.ninst = 0

    def _wait(self, ename, ev):
        e = self.eng[ename]
        if ev.key == "pe" and ename == "pe":
            return
        if ev.val is None:
            raise RuntimeError(f"wait on unresolved event ({ev.key}) from {ename}")
        if e["seen"].get(ev.key, 0) >= ev.val:
            return
        if SKIP_OLD_SELF and ev.key == ename and ev.val <= e["n"] - 2:
            return
        e["h"].wait_ge(ev.sem, ev.val)
        e["seen"][ev.key] = ev.val

    def _deps(self, ename, r, w):
        for t in r:
            if t.w is not None:
                self._wait(ename, t.w)
        for t in w:
            if t.w is not None:
                self._wait(ename, t.w)
            for ev in t.rs:
                self._wait(ename, ev)

    def _update(self, ev, r, w):
        for t in r:
            t.rs.append(ev)
        for t in w:
            t.w = ev
            t.rs = []

    def op(self, ename, fn, r=(), w=(), signal=True):
        e = self.eng[ename]
        self._deps(ename, r, w)
        ins = fn(e["h"])
        self.ninst += 1
        ev = Ev(ename, e["sem"], None)
        if signal:
            e["n"] += 1
            ins.then_inc(e["sem"], 1)
            ev.val = e["n"]
            for p in e["pending"]:
                p.val = e["n"]
            e["pending"] = []
        else:
            e["pending"].append(ev)
        self._update(ev, r, w)
        return ins

    def dma(self, out, in_, r=(), w=(), q="sp"):
        e = self.eng[q]
        i = self.dnext
        self.dnext = (self.dnext + 1) % len(self.dsems)
        sem = self.dsems[i]
        if self.dcount[i] > 0:
            self._wait(q, Ev(f"d{i}", sem, 16 * self.dcount[i]))
        self._deps(q, r, w)
        ins = e["h"].dma_start(out=out, in_=in_)
        self.ninst += 1
        self.dcount[i] += 1
        ins.then_inc(sem, 16)
        ev = Ev(f"d{i}", sem, 16 * self.dcount[i])
        self._update(ev, r, w)
        return ev

    def wait_all_dma(self, q="sp"):
        for i, sem in enumerate(self.dsems):
            if self.dcount[i] > 0:
                self._wait(q, Ev(f"d{i}", sem, 16 * self.dcount[i]))


class T:
    def __init__(self, ap_src, name):
        self.t = ap_src
        self.k = Tok(name)

    def __getitem__(self, idx):
        return self.t[idx]


def build_program(S=SEQ, depth=DEPTH, stop_after=None):
    NT = S // 128
    NBLK = S // 512
    assert S % 512 == 0
    nc = bass.Bass("TRN2", target_bir_lowering=False)
    sc = Sched(nc)

    class _Stop(Exception):
        pass

    def dram(name, shape, dt, kind):
        return nc.dram_tensor(name, list(shape), dt, kind=kind).ap()

    x_in = dram("x", [S, D], F32, "ExternalInput")
    pos_d = dram("pos", [128, NT], I32, "ExternalInput")
    invf_d = dram("invf", [128, 32], F32, "ExternalInput")
    ident_d = dram("ident", [128, 128], F32, "ExternalInput")
    triu_d = dram("triu", [128, 128], F32, "ExternalInput")
    w_in_d = dram("w_in", [depth, D, DIN], F32, "ExternalInput")
    w_up_d = dram("w_up", [depth, 16, 256], F32, "ExternalInput")
    vecs_d = dram("vecs", [depth, 128, NVEC], F32, "ExternalInput")
    w_uq_d = dram("w_uq", [depth, 256, 768], F32, "ExternalInput")
    w_ukv_d = dram("w_ukv", [depth, 128, 1024], F32, "ExternalInput")
    qg_d = dram("q_head_g", [depth, 192], F32, "ExternalInput")
    kg_d = dram("k_head_g", [depth, 192], F32, "ExternalInput")
    w_out_d = dram("w_out", [depth, D, D], F32, "ExternalInput")
    y_out = dram("y", [S, D], F32, "ExternalOutput")
    scr = [dram("xs0", [S, D], F32, "Internal"), dram("xs1", [S, D], F32, "Internal")]

    xbufs = [x_in] + [scr[l % 2] for l in range(depth - 1)] + [y_out]
    xtok = {}

    def xk(buf_idx, t):
        if buf_idx == 0:
            key = ("in", t)
        elif buf_idx == depth:
            key = ("out", t)
        else:
            key = ("s", (buf_idx - 1) % 2, t)
        if key not in xtok:
            xtok[key] = Tok(f"x{key}")
        return xtok[key]

    def sb(name, shape, dt):
        return T(nc.alloc_sbuf_tensor("sb_" + name, list(shape), dt), name)

    def ps(name, shape, dt):
        return T(nc.alloc_psum_tensor("ps_" + name, list(shape), dt), name)

    Win = sb("Win", [128, 8, DIN], BF16)
    Wout = sb("Wout", [128, 8, D], BF16)
    Wuq = sb("Wuq", [128, 2, 768], BF16)
    Wukv = sb("Wukv", [128, 1024], BF16)
    Wup = sb("Wup", [128, 256], BF16)
    stg = [sb(f"stg{i}", [128, 628], F32) for i in range(2)]
    KT = sb("KT", [128, S], BF16)
    WukT = sb("WukT", [128, 4, 128], BF16)
    KTr = sb("KTr", [128, S], BF16)
    Vc = sb("Vc", [128, NT, 4, 129], BF16)
    rks = sb("rks", [128, NT, 4], F32)
    cosT = sb("cosT", [128, NT, 32], F32)
    sinT = sb("sinT", [128, NT, 32], F32)
    ident = sb("ident", [128, 128], BF16)
    triu4 = sb("triu4", [128, 4, 128], BF16)
    vecs = sb("vecs", [128, depth, NVEC], F32)
    negb = sb("negb", [128, 2], F32)
    qg = sb("qg", [128, 192], F32)
    kg = sb("kg", [128, 192], F32)
    xt = [sb(f"xt{i}", [128, D], F32) for i in range(1)]
    xr = [sb(f"xr{i}", [128, D], F32) for i in range(2)]
    hb = sb("hb", [128, D], BF16)
    hT = [sb(f"hT{i}", [128, 8, 128], BF16) for i in range(2)]
    junk = sb("junk", [128, 256], BF16)
    st_x = sb("st_x", [128, 4], F32)
    glrT = [sb(f"glrT{i}", [128, 128], BF16) for i in range(2)]
    lrb = sb("lrb", [128, 128], BF16)
    e1 = sb("e1", [128, 2, 128], F32)
    csb = sb("csb", [128, 2, 128], F32)
    ones128 = sb("ones128", [128, 128], F32)
    Epl = sb("Epl", [128, 2, 128], F32)
    Emi = sb("Emi", [128, 2, 128], F32)
    qdm = sb("qdm", [128, 2, 2, 128], BF16)
    hmask = sb("hmask", [128, 2], F32)
    kiT = sb("kiT", [128, 2, 128], BF16)
    ki = sb("ki", [128, 2, 128], BF16)
    kiA = sb("kiA", [128, 2, 128], BF16)
    kiB = sb("kiB", [128, 2, 128], BF16)
    vb = sb("vb", [128, 512], BF16)
    Am = sb("Am", [128, 4, 128], BF16)
    Sst = sb("Sst", [128, 2, 128], F32)
    Sd = sb("Sd", [128, 2, 128], F32)
    Sbf = sb("Sbf", [128, 2, 128], BF16)
    sg_e = sb("sg_e", [128, 512], F32)
    sgate = sb("sgate", [128, 512], BF16)
    st_g = sb("st_g", [128, 8], F32)
    ogl = sb("ogl", [128, 4, 128], BF16)
    st_c = sb("st_c", [128, 8], F32)
    cn = sb("cn", [128, 384], BF16)
    cnT = sb("cnT", [128, 3, 128], BF16)
    kpg = sb("kpg", [128, 64], F32)
    qn = sb("qn", [128, 4, 192], F32)
    qb = sb("qb", [128, 4, 256], BF16)
    rtmp = [sb(f"rtmp{i}", [128, 4, 32], F32) for i in range(4)]
    st_q = sb("st_q", [128, 8], F32)
    st_k = sb("st_k", [128, 8], F32)
    krb = sb("krb", [128, 128], BF16)
    QTn = sb("QTn", [128, 4, 512], BF16)
    QTr = sb("QTr", [128, 4, 512], BF16)
    smg = sb("smg", [128, 4, 512], BF16)
    mg_e = sb("mg_e", [128, 512], F32)
    PT = [sb(f"PT{i}", [128, 512], BF16) for i in range(4)]
    rinv = sb("rinv", [128, 4], F32)
    om = sb("om", [128, 4, 128], BF16)
    ogT = sb("ogT", [128, 8, 512], BF16)
    posf = sb("posf", [128, NT], F32)

    pbig = ps("pbig", [128, 1024], F32)
    pmm = ps("pmm", [128, 512], F32)
    ptr = ps("ptr", [128, 1024], BF16)
    pst = [ps(f"pst{i}", [128, 512], F32) for i in range(2)]
    pO = [ps(f"pO{i}", [128, 2, 256], F32) for i in range(2)]

    MUL, ADD, SUB = ALU.mult, ALU.add, ALU.subtract

    def load_const_bf16(dst_ap_fn, src_ap, n, dst_tok):
        sc.dma(stg[0][:, 0:n], src_ap, w=[stg[0].k])
        sc.op("dve", lambda e: e.tensor_copy(out=dst_ap_fn(), in_=stg[0][:, 0:n]), r=[stg[0].k], w=[dst_tok])

    load_const_bf16(lambda: ident[:], ident_d[:, :], 128, ident.k)
    sc.dma(stg[1][:, 0:128], triu_d[:, :], w=[stg[1].k])
    for h in range(4):
        sc.op("dve", lambda e, h=h: e.tensor_copy(out=triu4[:, h, :], in_=stg[1][:, 0:128]), r=[stg[1].k], w=[triu4.k])
    sc.dma(vecs[:], vecs_d.rearrange("l p v -> p l v"), w=[vecs.k])
    sc.op("pool", lambda e: e.memset(ones128[:], 1.0), w=[ones128.k])
    sc.op("pool", lambda e: e.memset(hmask[0:64, 0:1], 0.125), w=[hmask.k])
    sc.op("pool", lambda e: e.memset(hmask[64:128, 0:1], 0.0), w=[hmask.k])
    sc.op("pool", lambda e: e.memset(hmask[0:64, 1:2], 0.0), w=[hmask.k])
    sc.op("pool", lambda e: e.memset(hmask[64:128, 1:2], 0.125), w=[hmask.k])
    sc.op("pool", lambda e: e.memset(Vc[:, :, :, 128:129], 1.0), w=[Vc.k])
    sc.op("pool", lambda e: e.memset(KTr[64:128, :], 0.0), w=[KTr.k])
    sc.op("pool", lambda e: e.memset(Wup[:], 0.0), w=[Wup.k])
    sc.op("pool", lambda e: e.memset(kiA[:], 0.0), w=[kiA.k])
    sc.op("pool", lambda e: e.memset(kiB[:], 0.0), w=[kiB.k])
    sc.op("pool", lambda e: e.memset(qb[:, :, 192:256], 0.0), w=[qb.k])
    sc.op("pool", lambda e: e.memset(krb[:, 64:128], 0.0), w=[krb.k])
    sc.op("pool", lambda e: e.memset(QTr[64:128, :, :], 0.0), w=[QTr.k])

    posi = sb("posi", [128, NT], I32)
    sc.dma(posi[:], pos_d[:, :], w=[posi.k])
    sc.op("dve", lambda e: e.tensor_copy(out=posf[:], in_=posi[:]), r=[posi.k], w=[posf.k])
    invf = sb("invf", [128, 32], F32)
    sc.dma(invf[:], invf_d[:, :], w=[invf.k])
    _og32 = ogT.t[:].rearrange("p a b -> p (a b)").bitcast(F32)
    ang = T(_og32[:, 0:NT * 32].rearrange("p (t c) -> p t c", c=32), "ang")
    kk_ = T(_og32[:, 1024:1024 + NT * 32].rearrange("p (t c) -> p t c", c=32), "kk_")
    ang.k = ogT.k
    kk_.k = ogT.k
    for t in range(NT):
        sc.op("dve", lambda e, t=t: e.tensor_scalar(out=ang[:, t, :], in0=invf[:], scalar1=posf[:, t:t + 1],
                                                     scalar2=None, op0=MUL), r=[invf.k, posf.k], w=[ang.k])
    MAGIC = 12582912.0
    sc.op("dve", lambda e: e.tensor_scalar(out=kk_[:], in0=ang[:], scalar1=1.0 / TWO_PI, scalar2=MAGIC,
                                            op0=MUL, op1=ADD), r=[ang.k], w=[kk_.k])
    sc.op("dve", lambda e: e.tensor_scalar(out=kk_[:], in0=kk_[:], scalar1=MAGIC, scalar2=None, op0=SUB),
          r=[kk_.k], w=[kk_.k])
    sc.op("dve", lambda e: e.scalar_tensor_tensor(out=ang[:], in0=kk_[:], scalar=-CW1, in1=ang[:], op0=MUL, op1=ADD),
          r=[kk_.k, ang.k], w=[ang.k])
    sc.op("dve", lambda e: e.scalar_tensor_tensor(out=ang[:], in0=kk_[:], scalar=-CW2, in1=ang[:], op0=MUL, op1=ADD),
          r=[kk_.k, ang.k], w=[ang.k])
    sc.op("dve", lambda e: e.tensor_scalar(out=kk_[:], in0=ang[:], scalar1=PI / 2, scalar2=None, op0=ADD),
          r=[ang.k], w=[kk_.k])
    wrapm = cosT
    sc.op("dve", lambda e: e.tensor_scalar(out=wrapm[:], in0=kk_[:], scalar1=PI, scalar2=None, op0=ALU.is_gt),
          r=[kk_.k], w=[wrapm.k])
    sc.op("dve", lambda e: e.scalar_tensor_tensor(out=kk_[:], in0=wrapm[:], scalar=-TWO_PI, in1=kk_[:], op0=MUL, op1=ADD),
          r=[wrapm.k, kk_.k], w=[kk_.k])
    for tt_ in (ang, kk_):
        sc.op("dve", lambda e, tt_=tt_: e.tensor_scalar(out=tt_[:], in0=tt_[:], scalar1=PI, scalar2=-PI,
                                                         op0=ALU.min, op1=ALU.max), r=[tt_.k], w=[tt_.k])
    sc.op("act", lambda e: e.activation(out=sinT[:], in_=ang[:], func=AF.Sin), r=[ang.k], w=[sinT.k])
    sc.op("act", lambda e: e.activation(out=cosT[:], in_=kk_[:], func=AF.Sin), r=[kk_.k], w=[cosT.k])

    stg_i = [0]

    def conv_weight(dst_ap, src_ap, n, scale_ap, dst_tok, eng):
        s = stg[stg_i[0] % 2]
        stg_i[0] += 1
        sc.dma(s[0:src_ap.shape[0], 0:n], src_ap, w=[s.k])
        p = src_ap.shape[0]
        if scale_ap is None:
            sc.op(eng, lambda e: e.tensor_copy(out=dst_ap, in_=s[0:p, 0:n]), r=[s.k], w=[dst_tok])
        else:
            sc.op(eng, lambda e: e.tensor_scalar(out=dst_ap, in0=s[0:p, 0:n], scalar1=scale_ap, scalar2=1.0,
                                                 op0=MUL, op1=MUL), r=[s.k, vecs.k], w=[dst_tok])

    def gen_prep(l, which):
        if "win" in which:
            i = 0
            for kc in range(8):
                for q4 in range(4):
                    c0 = q4 * 628
                    conv_weight(Win[:, kc, c0:c0 + 628], w_in_d[l, kc * 128:(kc + 1) * 128, c0:c0 + 628], 628,
                                vecs[:, l, kc:kc + 1], Win.k, ("pool", "dve")[i % 2])
                    i += 1
                    yield
        if "small" in which:
            prep_small(l)
            yield
        if "wout" in which:
            i = 0
            for kc in range(8):
                sap = vecs[:, l, 10:11] if kc < 4 else None
                for hf in range(2):
                    conv_weight(Wout[:, kc, hf * 512:(hf + 1) * 512], w_out_d[l, kc * 128:(kc + 1) * 128, hf * 512:(hf + 1) * 512], 512,
                                sap, Wout.k, ("pool", "dve")[i % 2])
                    i += 1
                    yield

    def prep_win(l):
        i = 0
        for kc in range(8):
            for q4 in range(4):
                c0 = q4 * 628
                eng = ("pool", "dve")[i % 2]
                i += 1
                conv_weight(Win[:, kc, c0:c0 + 628], w_in_d[l, kc * 128:(kc + 1) * 128, c0:c0 + 628], 628,
                            vecs[:, l, kc:kc + 1], Win.k, eng)

    def prep_small(l):
        for kc in range(2):
            for hf in range(2):
                conv_weight(Wuq[:, kc, hf * 384:(hf + 1) * 384], w_uq_d[l, kc * 128:(kc + 1) * 128, hf * 384:(hf + 1) * 384], 384,
                            vecs[:, l, 11 + kc:12 + kc], Wuq.k, "pool")
        for hf in range(2):
            conv_weight(Wukv[:, hf * 512:(hf + 1) * 512], w_ukv_d[l, :, hf * 512:(hf + 1) * 512], 512, vecs[:, l, 13:14], Wukv.k, "pool")
        conv_weight(Wup[0:16, :], w_up_d[l, :, :], 256, None, Wup.k, "pool")
        sc.op("pool", lambda e: e.tensor_scalar(out=negb[:], in0=vecs[:, l, 8:10], scalar1=-1.0, scalar2=1.0,
                                                op0=MUL, op1=MUL), r=[vecs.k], w=[negb.k])
        sc.dma(qg[:], qg_d[l, :].partition_broadcast(128), w=[qg.k])
        sc.dma(kg[:], kg_d[l, :].partition_broadcast(128), w=[kg.k])

    def prep_wukt(l):
        for h in range(4):
            sc.op("pe", lambda e, h=h: e.transpose(ptr[:, h * 128:(h + 1) * 128], Wukv[:, h * 256:h * 256 + 128], ident[:]),
                  r=[Wukv.k, ident.k], w=[ptr.k], signal=(h == 3))
        sc.op("dve", lambda e: e.tensor_scalar(out=WukT[:], in0=ptr[:, 0:512].rearrange("p (h c) -> p h c", h=4),
                                               scalar1=vecs[:, l, 14:15], scalar2=None, op0=MUL),
              r=[ptr.k, vecs.k], w=[WukT.k])

    def prep_wout(l):
        i = 0
        for kc in range(8):
            sap = vecs[:, l, 10:11] if kc < 4 else None
            for hf in range(2):
                conv_weight(Wout[:, kc, hf * 512:(hf + 1) * 512], w_out_d[l, kc * 128:(kc + 1) * 128, hf * 512:(hf + 1) * 512], 512,
                            sap, Wout.k, ("pool", "dve")[i % 2])
                i += 1

    def rstd_from_ss(st, c0, c1, n_elems, extra_bias_ln=None):
        sc.op("act", lambda e: e.activation(out=st[:, c0:c1], in_=st[:, c0:c1], func=AF.Ln, scale=1.0 / n_elems, bias=EPS),
              r=[st.k], w=[st.k])
        if extra_bias_ln is None:
            sc.op("act", lambda e: e.activation(out=st[:, c0:c1], in_=st[:, c0:c1], func=AF.Exp, scale=-0.5),
                  r=[st.k], w=[st.k])
        else:
            sc.op("act", lambda e: e.activation(out=st[:, c0:c1], in_=st[:, c0:c1], func=AF.Exp, scale=-0.5,
                                                bias=extra_bias_ln), r=[st.k], w=[st.k])

    def transposes(srcs, dst_views, n_rows_list):
        pass

    class V_:
        def __init__(self, ap, tok):
            self.t, self.k = ap, tok

        def __getitem__(self, idx):
            return self.t[idx]

    G0, G1 = pst[0], pst[1]
    GT = V_(pO[0].t[:].rearrange("p a b -> p (a b)").bitcast(BF16), pO[0].k)
    PN = V_(pO[1].t[:].rearrange("p a b -> p (a b)").bitcast(BF16), pO[1].k)
    junkG = sb("junkG", [128, 256], BF16)
    Ocp = sb("Ocp", [128, 4, 129], F32)

    def stage_load_x(l, t, buf):
        sc.dma(buf[:], xbufs[l][t * 128:(t + 1) * 128, :], r=[xk(l, t)], w=[buf.k])

    def gen_norm(l, t):
        xb_ = xt[0]
        h_T = hT[t % 2]
        sc.op("act", lambda e: e.activation(out=hb[:], in_=xb_[:], func=AF.Square, accum_out=st_x[:, 0:1]),
              r=[xb_.k], w=[hb.k, st_x.k])
        yield
        rstd_from_ss(st_x, 0, 1, D)
        yield
        sc.op("dve", lambda e: e.tensor_scalar(out=hb[:], in0=xb_[:], scalar1=st_x[:, 0:1], scalar2=None, op0=MUL),
              r=[xb_.k, st_x.k], w=[hb.k])
        if t + 1 < NT:
            stage_load_x(l, t + 1, xt[0])
        yield
        for kc in range(8):
            sc.op("pe", lambda e, kc=kc: e.transpose(PN[:, kc * 128:(kc + 1) * 128], hb[:, kc * 128:(kc + 1) * 128], ident[:]),
                  r=[hb.k, ident.k], w=[PN.k], signal=(kc == 7))
        yield
        sc.op("dve", lambda e: e.tensor_copy(out=h_T[:], in_=PN[:, :].rearrange("p (k c) -> p k c", k=8)),
              r=[PN.k], w=[h_T.k])
        yield
        PNf = pO[1].t[:].rearrange("p a b -> p (a b)")
        for kc in range(8):
            sc.op("pe", lambda e, kc=kc: e.matmul(PNf[:, 0:128], lhsT=h_T[:, kc, :], rhs=Win[:, kc, C_LR:C_LR + 128],
                                                  start=(kc == 0), stop=(kc == 7)),
                  r=[h_T.k, Win.k], w=[PN.k], signal=(kc == 7))
        yield
        sc.op("act", lambda e: e.copy(out=lrb[:], in_=PNf[:, 0:128]), r=[PN.k], w=[lrb.k])
        yield
        sc.op("pe", lambda e: e.transpose(PN[:, 512:640], lrb[:], ident[:]), r=[lrb.k, ident.k], w=[PN.k], signal=True)
        yield
        sc.op("dve", lambda e: e.tensor_copy(out=glrT[t % 2][:], in_=PN[:, 512:640]), r=[PN.k], w=[glrT[t % 2].k])
        yield

    def inproj_tok(h_T, c0, n, out_ap, out_tok):
        for kc in range(8):
            sc.op("pe", lambda e, kc=kc: e.matmul(out_ap, lhsT=h_T[:, kc, :], rhs=Win[:, kc, c0:c0 + n],
                                                  start=(kc == 0), stop=(kc == 7)),
                  r=[h_T.k, Win.k], w=[out_tok], signal=(kc == 7))

    def gen_silu_gate(src_ps, c0, tmp, dst_ap, dst_tok):
        sc.op("act", lambda e: e.activation(out=tmp[:], in_=src_ps[:, c0:c0 + 512], func=AF.Exp, scale=-1.0),
              r=[src_ps.k], w=[tmp.k])
        yield
        sc.op("act", lambda e: e.activation(out=tmp[:], in_=tmp[:], func=AF.Ln, bias=1.0), r=[tmp.k], w=[tmp.k])
        yield
        sc.op("act", lambda e: e.activation(out=tmp[:], in_=tmp[:], func=AF.Exp, scale=-1.0), r=[tmp.k], w=[tmp.k])
        yield
        sc.op("dve", lambda e: e.tensor_tensor(out=dst_ap, in0=src_ps[:, c0:c0 + 512], in1=tmp[:], op=MUL),
              r=[src_ps.k, tmp.k], w=[dst_tok])
        yield

    GTf = V_(pO[0].t[:].rearrange("p a b -> p (a b)"), pO[0].k)

    def gen_gla_gates(l, t):
        for c2 in range(2):
            sc.op("pe", lambda e, c2=c2: e.matmul(GTf[:, 256 + c2 * 128:256 + (c2 + 1) * 128],
                                                  lhsT=Wup[:, c2 * 128:(c2 + 1) * 128], rhs=glrT[t % 2][:], start=True, stop=True),
                  r=[Wup.k, glrT[t % 2].k], w=[GTf.k], signal=(c2 == 1))
        yield
        for c2 in range(2):
            sc.op("act", lambda e, c2=c2: e.activation(out=e1[:, c2, :], in_=GTf[:, 256 + c2 * 128:256 + (c2 + 1) * 128],
                                                       func=AF.Exp, scale=-1.0, bias=negb[:, c2:c2 + 1]),
                  r=[GTf.k, negb.k], w=[e1.k])
        yield
        sc.op("act", lambda e: e.activation(out=e1[:], in_=e1[:], func=AF.Ln, bias=1.0), r=[e1.k], w=[e1.k])
        yield
        for c2 in range(2):
            sc.op("dve", lambda e, c2=c2: e.tensor_tensor_scan(out=csb[:, c2, :], data0=ones128[:], data1=e1[:, c2, :],
                                                               initial=0.0, op0=MUL, op1=ADD),
                  r=[ones128.k, e1.k], w=[csb.k])
        yield
        sc.op("act", lambda e: e.activation(out=Epl[:], in_=csb[:], func=AF.Exp, scale=-1.0 / 16.0), r=[csb.k], w=[Epl.k])
        sc.op("act", lambda e: e.activation(out=Emi[:], in_=csb[:], func=AF.Exp, scale=1.0 / 16.0), r=[csb.k], w=[Emi.k])
        yield

    PNf32 = V_(pO[1].t[:].rearrange("p a b -> p (a b)"), pO[1].k)

    def gen_gate_g(l, t):
        h_T = hT[t % 2]
        inproj_tok(h_T, C_GG, 512, PNf32[:, 0:512], PNf32.k)
        yield 1
        yield from gen_silu_gate(PNf32, 0, sg_e, sgate[:], sgate.k)

    def gen_gate_m(l, t):
        h_T = hT[t % 2]
        ti = t % 4
        inproj_tok(h_T, C_MG, 512, PNf32[:, 0:512], PNf32.k)
        yield 1
        yield from gen_silu_gate(PNf32, 0, mg_e, smg[:, ti, :], smg.k)

    def gen_gates(l, t):
        yield from gen_gate_g(l, t)
        yield from gen_gate_m(l, t)

    def gen_norm_then_gates(l, t):
        if VARIANT != "glast":
            yield from gen_gate_g(l, t)
            if t + 1 < NT:
                yield from gen_norm(l, t + 1)
            yield from gen_gate_m(l, t)
        else:
            if t + 1 < NT:
                yield from gen_norm(l, t + 1)
            yield from gen_gates(l, t)

    def gen_gla(l, t):
        h_T = hT[t % 2]
        tc0 = (t % 4) * 128
        for c in range(4):
            col = (C_GQ if c < 2 else C_GK) + (c % 2) * 128
            for kc in range(8):
                sc.op("pe", lambda e, kc=kc, c=c, col=col: e.matmul(G0[:, c * 128:(c + 1) * 128],
                                                                     lhsT=Win[:, kc, col:col + 128], rhs=h_T[:, kc, :],
                                                                     start=(kc == 0), stop=(kc == 7)),
                      r=[h_T.k, Win.k], w=[G0.k], signal=(kc == 7 and c == 3))
            yield
        inproj_tok(h_T, C_GV, 512, G1[:, 0:512], G1.k)
        yield 1
        sc.op("act", lambda e: e.copy(out=vb[:], in_=G1[:, 0:512]), r=[G1.k], w=[vb.k])
        yield
        for hh in range(2):
            sc.op("dve", lambda e, hh=hh: e.scalar_tensor_tensor(out=qdm[:, :, hh, :],
                                                                 in0=G0[:, 0:256].rearrange("p (c t) -> p c t", c=2),
                                                                 scalar=hmask[:, hh:hh + 1], in1=Epl[:], op0=MUL, op1=MUL),
                  r=[G0.k, Epl.k, hmask.k], w=[qdm.k])
        sc.op("dve", lambda e: e.tensor_tensor(out=kiT[:], in0=G0[:, 256:512].rearrange("p (c t) -> p c t", c=2),
                                               in1=Emi[:], op=MUL), r=[G0.k, Emi.k], w=[kiT.k])
        yield
        for c2 in range(2):
            sc.op("pe", lambda e, c2=c2: e.transpose(GT[:, c2 * 128:(c2 + 1) * 128], kiT[:, c2, :], ident[:]),
                  r=[kiT.k, ident.k], w=[GT.k], signal=(c2 == 1))
        for h in range(4):
            c2, hh = h // 2, h % 2
            sc.op("pe", lambda e, h=h, c2=c2, hh=hh: e.matmul(G0[:, h * 128:(h + 1) * 128], lhsT=kiT[:, c2, :],
                                                              rhs=qdm[:, c2, hh, :], start=True, stop=True),
                  r=[kiT.k, qdm.k], w=[G0.k], signal=(h == 3))
        yield
        sc.op("dve", lambda e: e.tensor_copy(out=ki[:], in_=GT[:, 0:256].rearrange("p (c t) -> p c t", c=2)), r=[GT.k], w=[ki.k])
        sc.op("pool", lambda e: e.tensor_copy(out=kiA[:, :, 0:64], in_=ki[:, :, 0:64]), r=[ki.k], w=[kiA.k])
        sc.op("pool", lambda e: e.tensor_copy(out=kiB[:, :, 64:128], in_=ki[:, :, 64:128]), r=[ki.k], w=[kiB.k])
        sc.op("dve", lambda e: e.tensor_tensor(out=Am[:], in0=G0[:, 0:512].rearrange("p (h t) -> p h t", h=4),
                                               in1=triu4[:], op=MUL), r=[G0.k, triu4.k], w=[Am.k])
        yield
        for h in range(4):
            c2, hh = h // 2, h % 2
            sc.op("pe", lambda e, h=h: e.matmul(G1[:, h * 128:(h + 1) * 128], lhsT=Am[:, h, :],
                                                rhs=vb[:, h * 128:(h + 1) * 128], start=True, stop=False),
                  r=[Am.k, vb.k], w=[G1.k], signal=False)
            sc.op("pe", lambda e, h=h, c2=c2, hh=hh: e.matmul(G1[:, h * 128:(h + 1) * 128], lhsT=qdm[:, c2, hh, :],
                                                              rhs=Sbf[:, c2, :], start=False, stop=True),
                  r=[qdm.k, Sbf.k], w=[G1.k], signal=(h == 3))
        for c2 in range(2):
            sc.op("pe", lambda e, c2=c2: e.matmul(G0[:, c2 * 128:(c2 + 1) * 128], lhsT=kiA[:, c2, :],
                                                  rhs=vb[:, (2 * c2) * 128:(2 * c2 + 1) * 128], start=True, stop=False),
                  r=[kiA.k, vb.k], w=[G0.k], signal=False)
            sc.op("pe", lambda e, c2=c2: e.matmul(G0[:, c2 * 128:(c2 + 1) * 128], lhsT=kiB[:, c2, :],
                                                  rhs=vb[:, (2 * c2 + 1) * 128:(2 * c2 + 2) * 128], start=False, stop=True),
                  r=[kiB.k, vb.k], w=[G0.k], signal=(c2 == 1))
        yield
        for h in range(4):
            sc.op("act", lambda e, h=h: e.activation(out=junkG[:, 0:128], in_=G1[:, h * 128:(h + 1) * 128], func=AF.Square,
                                                     accum_out=st_g[:, h:h + 1]), r=[G1.k], w=[junkG.k, st_g.k])
        yield
        sc.op("dve", lambda e: e.tensor_tensor(out=Sd[:], in0=G0[:, 0:256].rearrange("p (c t) -> p c t", c=2), in1=Sst[:], op=ADD),
              r=[G0.k, Sst.k], w=[Sd.k])
        yield
        for c2 in range(2):
            sc.op("pool", lambda e, c2=c2: e.tensor_scalar(out=Sst[:, c2, :], in0=Sd[:, c2, :], scalar1=Epl[:, c2, 127:128],
                                                           scalar2=1.0, op0=MUL, op1=MUL), r=[Sd.k, Epl.k], w=[Sst.k])
        sc.op("pool", lambda e: e.tensor_copy(out=Sbf[:], in_=Sst[:]), r=[Sst.k], w=[Sbf.k])
        yield
        rstd_from_ss(st_g, 0, 4, 128)
        yield
        for h in range(4):
            sc.op("dve", lambda e, h=h: e.scalar_tensor_tensor(out=ogl[:, h, :], in0=G1[:, h * 128:(h + 1) * 128],
                                                               scalar=st_g[:, h:h + 1], in1=sgate[:, h * 128:(h + 1) * 128],
                                                               op0=MUL, op1=MUL), r=[G1.k, st_g.k, sgate.k], w=[ogl.k])
        yield
        for h in range(4):
            sc.op("pe", lambda e, h=h: e.transpose(GT[:, h * 128:(h + 1) * 128], ogl[:, h, :], ident[:]),
                  r=[ogl.k, ident.k], w=[GT.k], signal=(h == 3))
        yield
        sc.op("dve", lambda e: e.tensor_copy(out=ogT[:, 0:4, tc0:tc0 + 128],
                                             in_=GT[:, 0:512].rearrange("p (h c) -> p h c", h=4)),
              r=[GT.k], w=[ogT.k])
        yield

    def gen_mla(l, t):
        h_T = hT[t % 2]
        ti = t % 4
        tc0 = ti * 128
        inproj_tok(h_T, C_CQ, 448, pmm[:, 0:448], pmm.k)
        yield 1
        sc.op("act", lambda e: e.activation(out=junk[:, 0:256], in_=pmm[:, 0:256], func=AF.Square, accum_out=st_c[:, 0:1]),
              r=[pmm.k], w=[junk.k, st_c.k])
        yield
        sc.op("act", lambda e: e.activation(out=junk[:, 0:128], in_=pmm[:, 256:384], func=AF.Square, accum_out=st_c[:, 1:2]),
              r=[pmm.k], w=[junk.k, st_c.k])
        yield
        sc.op("act", lambda e: e.activation(out=junk[:, 0:64], in_=pmm[:, 384:448], func=AF.Square, accum_out=st_c[:, 2:3]),
              r=[pmm.k], w=[junk.k, st_c.k])
        yield
        rstd_from_ss(st_c, 0, 1, 256)
        yield
        rstd_from_ss(st_c, 1, 2, 128)
        yield
        sc.op("dve", lambda e: e.tensor_scalar(out=cn[:, 0:256], in0=pmm[:, 0:256], scalar1=st_c[:, 0:1], scalar2=None, op0=MUL),
              r=[pmm.k, st_c.k], w=[cn.k])
        sc.op("dve", lambda e: e.tensor_scalar(out=cn[:, 256:384], in0=pmm[:, 256:384], scalar1=st_c[:, 1:2], scalar2=None, op0=MUL),
              r=[pmm.k, st_c.k], w=[cn.k])
        sc.op("dve", lambda e: e.tensor_tensor(out=kpg[:], in0=pmm[:, 384:448], in1=kg[:, 128:192], op=MUL),
              r=[pmm.k, kg.k], w=[kpg.k])
        yield
        for c in range(3):
            sc.op("pe", lambda e, c=c: e.transpose(ptr[:, c * 128:(c + 1) * 128], cn[:, c * 128:(c + 1) * 128], ident[:]),
                  r=[cn.k, ident.k], w=[ptr.k], signal=(c == 2))
        yield
        sc.op("dve", lambda e: e.tensor_copy(out=cnT[:], in_=ptr[:, 0:384].rearrange("p (c t) -> p c t", c=3)),
              r=[ptr.k], w=[cnT.k])
        sc.op("pool", lambda e: e.tensor_copy(out=KT[:, t * 128:(t + 1) * 128], in_=cnT[:, 2, :]), r=[cnT.k], w=[KT.k])
        yield
        c1_, s1_ = cosT[:, t, :], sinT[:, t, :]
        a1, a2 = kpg[:, 0:32], kpg[:, 32:64]
        r0_, r1_, r2_, r3_ = (rtmp[i][:, 0, :] for i in range(4))
        sc.op("pool", lambda e: e.tensor_tensor(out=r0_, in0=a1, in1=c1_, op=MUL), r=[kpg.k, cosT.k], w=[rtmp[0].k])
        sc.op("pool", lambda e: e.tensor_tensor(out=r1_, in0=a2, in1=s1_, op=MUL), r=[kpg.k, sinT.k], w=[rtmp[1].k])
        sc.op("pool", lambda e: e.tensor_tensor(out=krb[:, 0:32], in0=r0_, in1=r1_, op=SUB), r=[rtmp[0].k, rtmp[1].k], w=[krb.k])
        sc.op("pool", lambda e: e.tensor_tensor(out=r2_, in0=a1, in1=s1_, op=MUL), r=[kpg.k, sinT.k], w=[rtmp[2].k])
        sc.op("pool", lambda e: e.tensor_tensor(out=r3_, in0=a2, in1=c1_, op=MUL), r=[kpg.k, cosT.k], w=[rtmp[3].k])
        sc.op("pool", lambda e: e.tensor_tensor(out=krb[:, 32:64], in0=r2_, in1=r3_, op=ADD), r=[rtmp[2].k, rtmp[3].k], w=[krb.k])
        for (o0, n) in ((0, 512), (512, 256)):
            for kc in range(2):
                sc.op("pe", lambda e, kc=kc, o0=o0, n=n: e.matmul(pbig[:, o0:o0 + n], lhsT=cnT[:, kc, :],
                                                                  rhs=Wuq[:, kc, o0:o0 + n], start=(kc == 0), stop=(kc == 1)),
                      r=[cnT.k, Wuq.k], w=[pbig.k], signal=(kc == 1 and o0 == 512))
        yield 1
        for h in range(4):
            sc.op("act", lambda e, h=h: e.activation(out=junk[:, 0:192], in_=pbig[:, h * 192:(h + 1) * 192], func=AF.Square,
                                                     accum_out=st_q[:, h:h + 1]), r=[pbig.k], w=[junk.k, st_q.k])
            yield
        rstd_from_ss(st_q, 0, 4, 192)
        yield
        for h in range(4):
            sc.op("dve", lambda e, h=h: e.scalar_tensor_tensor(out=qn[:, h, :], in0=pbig[:, h * 192:(h + 1) * 192],
                                                               scalar=st_q[:, h:h + 1], in1=qg[:], op0=MUL, op1=MUL),
                  r=[pbig.k, st_q.k, qg.k], w=[qn.k])
            yield
        for half in range(2):
            sc.op("pe", lambda e, half=half: e.matmul(pbig[:, half * 512:(half + 1) * 512], lhsT=cnT[:, 2, :],
                                                      rhs=Wukv[:, half * 512:(half + 1) * 512], start=True, stop=True),
                  r=[cnT.k, Wukv.k], w=[pbig.k], signal=(half == 1))
        sc.op("pe", lambda e: e.transpose(ptr[:, 512:640], krb[:], ident[:]), r=[krb.k, ident.k], w=[ptr.k], signal=True)
        yield
        sc.op("dve", lambda e: e.tensor_copy(out=KTr[:, t * 128:(t + 1) * 128], in_=ptr[:, 512:640]), r=[ptr.k], w=[KTr.k])
        sc.op("pool", lambda e: e.tensor_copy(out=qb[:, :, 0:128], in_=qn[:, :, 0:128]), r=[qn.k], w=[qb.k])
        cos4 = cosT[:, t, :].unsqueeze(1).broadcast_to([128, 4, 32])
        sin4 = sinT[:, t, :].unsqueeze(1).broadcast_to([128, 4, 32])
        t1, t2 = qn[:, :, 128:160], qn[:, :, 160:192]
        sc.op("pool", lambda e: e.tensor_tensor(out=rtmp[0][:], in0=t1, in1=cos4, op=MUL), r=[qn.k, cosT.k], w=[rtmp[0].k])
        sc.op("pool", lambda e: e.tensor_tensor(out=rtmp[1][:], in0=t2, in1=sin4, op=MUL), r=[qn.k, sinT.k], w=[rtmp[1].k])
        yield
        sc.op("pool", lambda e: e.tensor_tensor(out=qb[:, :, 128:160], in0=rtmp[0][:], in1=rtmp[1][:], op=SUB),
              r=[rtmp[0].k, rtmp[1].k], w=[qb.k])
        sc.op("pool", lambda e: e.tensor_tensor(out=rtmp[2][:], in0=t1, in1=sin4, op=MUL), r=[qn.k, sinT.k], w=[rtmp[2].k])
        yield
        sc.op("pool", lambda e: e.tensor_tensor(out=rtmp[3][:], in0=t2, in1=cos4, op=MUL), r=[qn.k, cosT.k], w=[rtmp[3].k])
        sc.op("pool", lambda e: e.tensor_tensor(out=qb[:, :, 160:192], in0=rtmp[2][:], in1=rtmp[3][:], op=ADD),
              r=[rtmp[2].k, rtmp[3].k], w=[qb.k])
        yield
        kv4 = pbig[:, :].rearrange("p (h c) -> p h c", h=4)
        for h in range(4):
            sc.op("act", lambda e, h=h: e.activation(out=junk[:, 0:128], in_=pbig[:, h * 256:h * 256 + 128], func=AF.Square,
                                                     accum_out=st_k[:, h:h + 1]), r=[pbig.k], w=[junk.k, st_k.k])
            yield
        sc.op("act", lambda e: e.copy(out=Vc[:, t, :, 0:128], in_=kv4[:, :, 128:256]), r=[pbig.k], w=[Vc.k])
        sc.op("dve", lambda e: e.tensor_scalar(out=st_k[:, 0:4], in0=st_k[:, 0:4], scalar1=st_c[:, 2:3], scalar2=None, op0=ADD),
              r=[st_k.k, st_c.k], w=[st_k.k])
        yield
        rstd_from_ss(st_k, 0, 4, 192, extra_bias_ln=float(np.log(192.0 ** -0.5)))
        yield
        sc.op("pool", lambda e: e.tensor_copy(out=rks[:, t, :], in_=st_k[:, 0:4]), r=[st_k.k], w=[rks.k])
        for h in range(4):
            sc.op("pe", lambda e, h=h: e.transpose(ptr[:, h * 128:(h + 1) * 128], qb[:, h, 0:128], ident[:]),
                  r=[qb.k, ident.k], w=[ptr.k], signal=False)
        for h in range(4):
            sc.op("pe", lambda e, h=h: e.transpose(ptr[:, 512 + h * 128:512 + (h + 1) * 128], qb[:, h, 128:256], ident[:]),
                  r=[qb.k, ident.k], w=[ptr.k], signal=(h == 3))
        yield
        sc.op("dve", lambda e: e.tensor_copy(out=QTn[:, :, tc0:tc0 + 128], in_=ptr[:, 0:512].rearrange("p (h c) -> p h c", h=4)),
              r=[ptr.k], w=[QTn.k])
        sc.op("dve", lambda e: e.tensor_copy(out=QTr[:, :, tc0:tc0 + 128],
                                             in_=ptr[:, 512:1024].rearrange("p (h c) -> p h c", h=4)),
              r=[ptr.k], w=[QTr.k])
        yield

    step_ctr = [0]

    def run_interleaved(gens):
        gens = [g for g in gens if g is not None]
        sleep = {}
        lim = int(VARIANT.split(":")[1]) if VARIANT.startswith("steps:") else None
        while gens:
            progressed = False
            for g in list(gens):
                if sleep.get(id(g), 0) > 0 and len(gens) > 1:
                    sleep[id(g)] -= 1
                    continue
                if lim is not None and step_ctr[0] >= lim:
                    raise _Stop()
                step_ctr[0] += 1
                progressed = True
                try:
                    res = next(g)
                    if isinstance(res, tuple) and res[0] == "spawn":
                        gens.append(res[1])
                    elif isinstance(res, int) and PIPE_DIST:
                        sleep[id(g)] = res * PIPE_DIST
                except StopIteration:
                    while g in gens:
                        gens.remove(g)
            if not progressed:
                for k in sleep:
                    sleep[k] = 0

    pt_i = [0]
    ps_i = [0]

    def stage_attention(l, b, bg=None):
        outs = [(pst[0], pst[0][:, 0:512]), (pst[1], pst[1][:, 0:512]), (pmm, pmm[:, 0:512]), (pbig, pbig[:, 0:512])]
        for h in range(4):
            tk, ap = outs[h]
            sc.op("pe", lambda e, h=h, ap=ap: e.matmul(ap, lhsT=WukT[:, h, :], rhs=QTn[:, h, :], start=True, stop=True),
                  r=[WukT.k, QTn.k], w=[tk.k], signal=True)
        for h in range(4):
            tk, ap = outs[h]
            sc.op(("dve", "act")[h % 2], lambda e, h=h, ap=ap: (e.tensor_copy(out=QTn[:, h, :], in_=ap) if h % 2 == 0
                                                                 else e.copy(out=QTn[:, h, :], in_=ap)),
                  r=[tk.k], w=[QTn.k])
        nk = 4 * b + 4
        steps = [(h, j) for h in range(4) for j in range(nk)]
        bufs = {}

        def emit_st(i):
            h, j = steps[i]
            r = j - 4 * b
            q0 = 128 * r if r > 0 else 0
            n = 512 - q0
            pst_ = pst3[ps_i[0] % 3]
            ps_i[0] += 1
            bufs[i] = pst_
            sc.op("pe", lambda e: e.matmul(pst_[:, 0:n], lhsT=KT[:, j * 128:(j + 1) * 128], rhs=QTn[:, h, q0:512],
                                           start=True, stop=False), r=[KT.k, QTn.k], w=[pst_.k], signal=False)
            sc.op("pe", lambda e: e.matmul(pst_[:, 0:n], lhsT=KTr[:, j * 128:(j + 1) * 128], rhs=QTr[:, h, q0:512],
                                           start=False, stop=True), r=[KTr.k, QTr.k], w=[pst_.k], signal=True)

        first = [True, True]
        pst3 = [pst[0], pst[1], pmm]
        emit_st(0)
        if len(steps) > 1:
            emit_st(1)
        for i, (h, j) in enumerate(steps):
            if bg is not None and i % 2 == 1:
                next(bg, None)
            if i + 2 < len(steps):
                emit_st(i + 2)
            r = j - 4 * b
            q0 = 128 * r if r > 0 else 0
            n = 512 - q0
            pst_ = bufs.pop(i)
            PT_ = PT[pt_i[0] % 4]
            pt_i[0] += 1
            sc.op("act", lambda e: e.activation(out=PT_[:, 0:n], in_=pst_[:, 0:n], func=AF.Exp, scale=rks[:, j, h:h + 1]),
                  r=[pst_.k, rks.k], w=[PT_.k])
            if r >= 0:
                sc.op("pool", lambda e: e.tensor_tensor(out=PT_[:, 0:128], in0=PT_[:, 0:128], in1=triu4[:, 0, :], op=MUL),
                      r=[PT_.k, triu4.k], w=[PT_.k])
            if j == 0:
                first = [True, True]
            for s in range(max(r, 0), 4):
                bank = s // 2
                st_flag = first[bank]
                first[bank] = False
                last = (j == 4 * b + s)
                c0 = 128 * s - q0
                sc.op("pe", lambda e, s=s, bank=bank, st_flag=st_flag, last=last, c0=c0: e.matmul(
                    pO[bank][:, s % 2, 0:129], lhsT=PT_[:, c0:c0 + 128], rhs=Vc[:, j, h, :],
                    start=st_flag, stop=last, skip_group_check=True),
                    r=[PT_.k, Vc.k], w=[pO[bank].k], signal=(s == 3))
            if j == nk - 1:
                for bank in range(2):
                    sc.op("dve", lambda e, bank=bank: e.tensor_copy(out=Ocp[:, 2 * bank:2 * bank + 2, :],
                                                                    in_=pO[bank][:, :, 0:129]),
                          r=[pO[bank].k], w=[Ocp.k])
                sc.op("dve", lambda e: e.reciprocal(out=rinv[:, 0:4], in_=Ocp[:, :, 128]), r=[Ocp.k], w=[rinv.k])
                for s in range(4):
                    sc.op("dve", lambda e, s=s: e.scalar_tensor_tensor(out=om[:, s, :], in0=Ocp[:, s, 0:128],
                                                                       scalar=rinv[:, s:s + 1],
                                                                       in1=smg[:, s, h * 128:(h + 1) * 128], op0=MUL, op1=MUL),
                          r=[Ocp.k, rinv.k, smg.k], w=[om.k])
                for s in range(4):
                    sc.op("pe", lambda e, s=s: e.transpose(ptr[:, s * 128:(s + 1) * 128], om[:, s, :], ident[:]),
                          r=[om.k, ident.k], w=[ptr.k], signal=(s == 3))
                sc.op("dve", lambda e: e.tensor_copy(out=ogT[:, 4 + h, :], in_=ptr[:, 0:512]), r=[ptr.k], w=[ogT.k])

    def xr_load(l, t):
        xr_ = xr[t % 2]
        sc.dma(xr_[:], xbufs[l][t * 128:(t + 1) * 128, :], r=[xk(l, t)], w=[xr_.k])

    def stage_outproj(l, b):
        for s in range(4):
            t = 4 * b + s
            xr_ = xr[t % 2]
            for half in range(2):
                for kc in range(8):
                    sc.op("pe", lambda e, half=half, kc=kc: e.matmul(pbig[:, half * 512:(half + 1) * 512],
                                                                     lhsT=ogT[:, kc, s * 128:(s + 1) * 128],
                                                                     rhs=Wout[:, kc, half * 512:(half + 1) * 512],
                                                                     start=(kc == 0), stop=(kc == 7)),
                          r=[ogT.k, Wout.k], w=[pbig.k], signal=(kc == 7 and half == 1))
            sc.op("dve", lambda e: e.tensor_tensor(out=xr_[:], in0=pbig[:, :], in1=xr_[:], op=ADD),
                  r=[pbig.k, xr_.k], w=[xr_.k])
            sc.dma(xbufs[l + 1][t * 128:(t + 1) * 128, :], xr_[:], r=[xr_.k], w=[xk(l + 1, t)])
            if s + 2 < 4:
                xr_load(l, t + 2)

    def chk(name):
        if stop_after == name:
            raise _Stop()

    try:
        pending_wout = [None]
        chk("init")
        prep_win(0)
        prep_small(0)
        prep_wukt(0)
        prep_wout(0)
        chk("prep")
        for l in range(depth):
            sc.op("pool", lambda e: e.memset(Sst[:], 0.0), w=[Sst.k])
            sc.op("pool", lambda e: e.memset(Sbf[:], 0.0), w=[Sbf.k])
            stage_load_x(l, 0, xt[0])
            run_interleaved([gen_norm(l, 0)])
            chk("norm0")
            for t in range(NT):
                gm = gen_mla(l, t)
                extra = pending_wout[0]
                pending_wout[0] = None
                if VARIANT == "mla2":
                    run_interleaved([gm, gen_gla_gates(l, t), gen_gla(l, t), gm, gen_norm_then_gates(l, t)])
                else:
                    chains = {"a": gen_gla_gates(l, t), "g": gen_gla(l, t), "m": gm, "n": gen_norm_then_gates(l, t)}
                    order = VARIANT[4:] if VARIANT.startswith("ord:") else "agmn"
                    run_interleaved([chains[c] for c in order] + [extra])
                chk("mla")
                bg = None
                if t == NT - 1 and l + 1 < depth:
                    bg = gen_prep(l + 1, ("win", "small"))
                if t % 4 == 3:
                    b = t // 4
                    xr_load(l, 4 * b)
                    xr_load(l, 4 * b + 1)
                    if VARIANT != "noattn":
                        stage_attention(l, b, bg)
                    if bg is not None:
                        for _ in bg:
                            pass
                    chk("attn")
                    stage_outproj(l, b)
                    chk("out")
            if l + 1 < depth:
                prep_wukt(l + 1)
                pending_wout[0] = gen_prep(l + 1, ("wout",))
    except _Stop:
        pass
    for en in ("pe", "act", "dve", "pool"):
        e_ = sc.eng[en]
        if e_["n"] > 0:
            sc._wait("sp", Ev(en, e_["sem"], e_["n"]))
    sc.wait_all_dma("sp")
    nc._sched_ninst = sc.ninst
    return nc


def _host_inputs(inputs, S, depth):
    f32 = np.float32
    vecs = np.zeros((depth, 128, NVEC), f32)
    vecs[:, :, 0:8] = inputs["norm_g"][:depth].reshape(depth, 8, 128).transpose(0, 2, 1)
    vecs[:, :, 8:10] = inputs["b_gla_gate"][:depth].reshape(depth, 2, 128).transpose(0, 2, 1)
    vecs[:, :, 10] = inputs["gla_norm_g"][:depth]
    vecs[:, :, 11:13] = inputs["mla_q_norm_g"][:depth].reshape(depth, 2, 128).transpose(0, 2, 1)
    vecs[:, :, 13] = inputs["mla_kv_norm_g"][:depth]
    vecs[:, :, 14] = inputs["k_head_g"][:depth, 0:128]
    invf = (10000.0 ** (-np.arange(0, 64, 2, dtype=f32) / f32(64))).astype(f32)
    common = {
        "invf": np.ascontiguousarray(np.broadcast_to(invf[None, :], (128, 32))).astype(f32),
        "ident": np.eye(128, dtype=f32),
        "triu": np.triu(np.ones((128, 128), f32)),
        "w_in": np.ascontiguousarray(np.concatenate([inputs["w_in"][:depth, :, 0:1024], inputs["w_in"][:depth, :, 1040:1552],
                                                     inputs["w_in"][:depth, :, 1024:1040], inputs["w_in"][:depth, :, 1552:]], axis=2), dtype=f32),
        "w_up": np.ascontiguousarray(inputs["w_gla_gate_up"][:depth], dtype=f32),
        "vecs": vecs,
        "w_uq": np.ascontiguousarray(inputs["w_uq"][:depth], dtype=f32),
        "w_ukv": np.ascontiguousarray(inputs["w_ukv"][:depth], dtype=f32),
        "q_head_g": np.ascontiguousarray(inputs["q_head_g"][:depth], dtype=f32),
        "k_head_g": np.ascontiguousarray(inputs["k_head_g"][:depth], dtype=f32),
        "w_out": np.ascontiguousarray(inputs["w_out"][:depth], dtype=f32),
    }
    NT = S // 128
    in_maps = []
    for c in range(8):
        b = c % BATCH
        m = dict(common)
        m["x"] = np.ascontiguousarray(inputs["x"][b, :S], dtype=f32)
        m["pos"] = np.ascontiguousarray(inputs["positions"][b, :S].reshape(NT, 128).T.astype(np.int32))
        in_maps.append(m)
    return in_maps


_NC_CACHE = {}


def run(inputs, S=SEQ, depth=DEPTH):
    key = (S, depth)
    if key not in _NC_CACHE:
        _NC_CACHE[key] = build_program(S, depth)
    nc = _NC_CACHE[key]
    in_maps = _host_inputs(inputs, S, depth)
    res = run_bass_kernel_spmd(nc, in_maps, core_ids=list(range(8)))
    return np.stack([res.results[b]["y"] for b in range(BATCH)], axis=0)


def kernel(**inputs):
    inputs = {k: np.asarray(v) for k, v in inputs.items()}
    return run(inputs).astype(np.float32)
```

```python
import numpy as np
import concourse.bass as bass
import concourse.mybir as mybir
from concourse.bass_utils import run_bass_kernel_spmd

F32 = mybir.dt.float32
BF16 = mybir.dt.bfloat16
I32 = mybir.dt.int32
AF = mybir.ActivationFunctionType
ALU = mybir.AluOpType

D = 1024
DIN = 2512
DEPTH = 4
SEQ = 4096
BATCH = 4
EPS = 1e-6
C_GQ, C_GK, C_GV, C_GG, C_LR, C_CQ, C_CKV, C_KPE, C_MG = 0, 256, 512, 1024, 1536, 1552, 1808, 1936, 2000
TWO_PI = 6.283185307179586
CW1 = 6.28125
CW2 = TWO_PI - CW1
PI = 3.141592653589793
NVEC = 16
VARIANT = ""
SKIP_OLD_SELF = False
PIPE_DIST = 1
SLEEP_DEFAULT = "sn"


class Ev:
    __slots__ = ("key", "sem", "val")

    def __init__(self, key, sem, val):
        self.key, self.sem, self.val = key, sem, val


class Tok:
    __slots__ = ("name", "w", "rs")

    def __init__(self, name):
        self.name, self.w, self.rs = name, None, []


class Sched:
    def __init__(self, nc, n_dma_sems=14):
        self.nc = nc
        self.eng = {}
        for name, h in (("pe", nc.tensor), ("act", nc.scalar), ("dve", nc.vector),
                        ("pool", nc.gpsimd), ("sp", nc.sync)):
            self.eng[name] = dict(h=h, sem=nc.alloc_semaphore(name=f"s_{name}"), n=0, seen={}, pending=[])
        self.dsems = [nc.alloc_semaphore(name=f"s_dma{i}") for i in range(n_dma_sems)]
        self.dcount = [0] * n_dma_sems
        self.dnext = 0
        self.ninst = 0

    def _wait(self, ename, ev):
        e = self.eng[ename]
        if ev.key == "pe" and ename == "pe":
            return
        if ev.val is None:
            raise RuntimeError(f"wait on unresolved event ({ev.key}) from {ename}")
        if e["seen"].get(ev.key, 0) >= ev.val:
            return
        if SKIP_OLD_SELF and ev.key == ename and ev.val <= e["n"] - 2:
            return
        e["h"].wait_ge(ev.sem, ev.val)
        e["seen"][ev.key] = ev.val

    def _deps(self, ename, r, w):
        for t in r:
            if t.w is not None:
                self._wait(ename, t.w)
        for t in w:
            if t.w is not None:
                self._wait(ename, t.w)
            for ev in t.rs:
                self._wait(ename, ev)

    def _update(self, ev, r, w):
        for t in r:
            t.rs.append(ev)
        for t in w:
            t.w = ev
            t.rs = []

    def op(self, ename, fn, r=(), w=(), signal=True):
        e = self.eng[ename]
        self._deps(ename, r, w)
        ins = fn(e["h"])
        self.ninst += 1
        ev = Ev(ename, e["sem"], None)
        if signal:
            e["n"] += 1
            ins.then_inc(e["sem"], 1)
            ev.val = e["n"]
            for p in e["pending"]:
                p.val = e["n"]
            e["pending"] = []
        else:
            e["pending"].append(ev)
        self._update(ev, r, w)
        return ins

    def dma(self, out, in_, r=(), w=(), q="sp"):
        e = self.eng[q]
        i = self.dnext
        self.dnext = (self.dnext + 1) % len(self.dsems)
        sem = self.dsems[i]
        if self.dcount[i] > 0:
            self._wait(q, Ev(f"d{i}", sem, 16 * self.dcount[i]))
        self._deps(q, r, w)
        ins = e["h"].dma_start(out=out, in_=in_)
        self.ninst += 1
        self.dcount[i] += 1
        ins.then_inc(sem, 16)
        ev = Ev(f"d{i}", sem, 16 * self.dcount[i])
        self._update(ev, r, w)
        return ev

    def wait_all_dma(self, q="sp"):
        for i, sem in enumerate(self.dsems):
            if self.dcount[i] > 0:
                self._wait(q, Ev(f"d{i}", sem, 16 * self.dcount[i]))


class T:
    def __init__(self, ap_src, name):
        self.t = ap_src
        self.k = Tok(name)

    def __getitem__(self, idx):
        return self.t[idx]


def build_program(S=SEQ, depth=DEPTH, stop_after=None):
    NT = S // 128
    NBLK = S // 512
    assert S % 512 == 0
    nc = bass.Bass("TRN2", target_bir_lowering=False)
    sc = Sched(nc)

    class _Stop(Exception):
        pass

    SLP = {k: (1 if (VARIANT.startswith("sl:") and k in VARIANT[3:]) or k in SLEEP_DEFAULT else None) for k in "agmns"}

    def dram(name, shape, dt, kind):
        return nc.dram_tensor(name, list(shape), dt, kind=kind).ap()

    x_in = dram("x", [S, D], F32, "ExternalInput")
    pos_d = dram("pos", [128, NT], I32, "ExternalInput")
    invf_d = dram("invf", [128, 32], F32, "ExternalInput")
    ident_d = dram("ident", [128, 128], F32, "ExternalInput")
    triu_d = dram("triu", [128, 128], F32, "ExternalInput")
    w_in_d = dram("w_in", [depth, D, DIN], F32, "ExternalInput")
    w_up_d = dram("w_up", [depth, 16, 256], F32, "ExternalInput")
    vecs_d = dram("vecs", [depth, 128, NVEC], F32, "ExternalInput")
    w_uq_d = dram("w_uq", [depth, 256, 768], F32, "ExternalInput")
    w_ukv_d = dram("w_ukv", [depth, 128, 1024], F32, "ExternalInput")
    qg_d = dram("q_head_g", [depth, 192], F32, "ExternalInput")
    kg_d = dram("k_head_g", [depth, 192], F32, "ExternalInput")
    w_out_d = dram("w_out", [depth, D, D], F32, "ExternalInput")
    y_out = dram("y", [S, D], F32, "ExternalOutput")
    scr = [dram("xs0", [S, D], F32, "Internal"), dram("xs1", [S, D], F32, "Internal")]

    xbufs = [x_in] + [scr[l % 2] for l in range(depth - 1)] + [y_out]
    xtok = {}

    def xk(buf_idx, t):
        if buf_idx == 0:
            key = ("in", t)
        elif buf_idx == depth:
            key = ("out", t)
        else:
            key = ("s", (buf_idx - 1) % 2, t)
        if key not in xtok:
            xtok[key] = Tok(f"x{key}")
        return xtok[key]

    def sb(name, shape, dt):
        return T(nc.alloc_sbuf_tensor("sb_" + name, list(shape), dt), name)

    def ps(name, shape, dt):
        return T(nc.alloc_psum_tensor("ps_" + name, list(shape), dt), name)

    Win = sb("Win", [128, 8, DIN], BF16)
    Wout = sb("Wout", [128, 8, D], BF16)
    Wuq = sb("Wuq", [128, 2, 768], BF16)
    Wukv = sb("Wukv", [128, 1024], BF16)
    Wup = sb("Wup", [128, 256], BF16)
    stg = [sb(f"stg{i}", [128, 628], F32) for i in range(2)]
    KT = sb("KT", [128, S], BF16)
    WukT = sb("WukT", [128, 4, 128], BF16)
    KTr = sb("KTr", [128, S], BF16)
    Vc = sb("Vc", [128, NT, 4, 129], BF16)
    rks = sb("rks", [128, NT, 4], F32)
    cosT = sb("cosT", [128, NT, 32], F32)
    sinT = sb("sinT", [128, NT, 32], F32)
    ident = sb("ident", [128, 128], BF16)
    triu4 = sb("triu4", [128, 4, 128], BF16)
    vecs = sb("vecs", [128, depth, NVEC], F32)
    negb = sb("negb", [128, 2], F32)
    qg = sb("qg", [128, 192], F32)
    kg = sb("kg", [128, 192], F32)
    xt = [sb(f"xt{i}", [128, D], F32) for i in range(1)]
    xr = [sb(f"xr{i}", [128, D], F32) for i in range(2)]
    hb = sb("hb", [128, D], BF16)
    hT = [sb(f"hT{i}", [128, 8, 128], BF16) for i in range(2)]
    junk = sb("junk", [128, 256], BF16)
    st_x = sb("st_x", [128, 4], F32)
    glrT = [sb(f"glrT{i}", [128, 128], BF16) for i in range(2)]
    lrb = sb("lrb", [128, 128], BF16)
    e1 = sb("e1", [128, 2, 128], F32)
    csb = sb("csb", [128, 2, 128], F32)
    ones128 = sb("ones128", [128, 128], F32)
    Epl = sb("Epl", [128, 2, 128], F32)
    Emi = sb("Emi", [128, 2, 128], F32)
    qdm = sb("qdm", [128, 2, 2, 128], BF16)
    hmask = sb("hmask", [128, 2], F32)
    kiT = sb("kiT", [128, 2, 128], BF16)
    ki = sb("ki", [128, 2, 128], BF16)
    kiA = sb("kiA", [128, 2, 128], BF16)
    kiB = sb("kiB", [128, 2, 128], BF16)
    vb = sb("vb", [128, 512], BF16)
    Am = sb("Am", [128, 4, 128], BF16)
    Sst = sb("Sst", [128, 2, 128], F32)
    Sd = sb("Sd", [128, 2, 128], F32)
    Sbf = sb("Sbf", [128, 2, 128], BF16)
    sg_e = sb("sg_e", [128, 512], F32)
    sgate = sb("sgate", [128, 512], BF16)
    st_g = sb("st_g", [128, 8], F32)
    ogl = sb("ogl", [128, 4, 128], BF16)
    st_c = sb("st_c", [128, 8], F32)
    cn = sb("cn", [128, 384], BF16)
    cnT = sb("cnT", [128, 3, 128], BF16)
    kpg = sb("kpg", [128, 64], F32)
    qn = sb("qn", [128, 4, 192], F32)
    qb = sb("qb", [128, 4, 256], BF16)
    rtmp = [sb(f"rtmp{i}", [128, 4, 32], F32) for i in range(4)]
    st_q = sb("st_q", [128, 8], F32)
    st_k = sb("st_k", [128, 8], F32)
    krb = sb("krb", [128, 128], BF16)
    QTn = sb("QTn", [128, 4, 512], BF16)
    QTr = sb("QTr", [128, 4, 512], BF16)
    smg = sb("smg", [128, 4, 512], BF16)
    mg_e = sb("mg_e", [128, 512], F32)
    PT = [sb(f"PT{i}", [128, 512], BF16) for i in range(4)]
    rinv = sb("rinv", [128, 4], F32)
    om = sb("om", [128, 4, 128], BF16)
    ogT = sb("ogT", [128, 8, 512], BF16)
    posf = sb("posf", [128, NT], F32)

    pbig = ps("pbig", [128, 1024], F32)
    pmm = ps("pmm", [128, 512], F32)
    ptr = ps("ptr", [128, 1024], BF16)
    pst = [ps(f"pst{i}", [128, 512], F32) for i in range(2)]
    pO = [ps(f"pO{i}", [128, 2, 256], F32) for i in range(2)]

    MUL, ADD, SUB = ALU.mult, ALU.add, ALU.subtract

    def load_const_bf16(dst_ap_fn, src_ap, n, dst_tok):
        sc.dma(stg[0][:, 0:n], src_ap, w=[stg[0].k])
        sc.op("dve", lambda e: e.tensor_copy(out=dst_ap_fn(), in_=stg[0][:, 0:n]), r=[stg[0].k], w=[dst_tok])

    load_const_bf16(lambda: ident[:], ident_d[:, :], 128, ident.k)
    sc.dma(stg[1][:, 0:128], triu_d[:, :], w=[stg[1].k])
    for h in range(4):
        sc.op("dve", lambda e, h=h: e.tensor_copy(out=triu4[:, h, :], in_=stg[1][:, 0:128]), r=[stg[1].k], w=[triu4.k])
    sc.dma(vecs[:], vecs_d.rearrange("l p v -> p l v"), w=[vecs.k])
    sc.op("pool", lambda e: e.memset(ones128[:], 1.0), w=[ones128.k])
    sc.op("pool", lambda e: e.memset(hmask[0:64, 0:1], 0.125), w=[hmask.k])
    sc.op("pool", lambda e: e.memset(hmask[64:128, 0:1], 0.0), w=[hmask.k])
    sc.op("pool", lambda e: e.memset(hmask[0:64, 1:2], 0.0), w=[hmask.k])
    sc.op("pool", lambda e: e.memset(hmask[64:128, 1:2], 0.125), w=[hmask.k])
    sc.op("pool", lambda e: e.memset(Vc[:, :, :, 128:129], 1.0), w=[Vc.k])
    sc.op("pool", lambda e: e.memset(KTr[64:128, :], 0.0), w=[KTr.k])
    sc.op("pool", lambda e: e.memset(Wup[:], 0.0), w=[Wup.k])
    sc.op("pool", lambda e: e.memset(kiA[:], 0.0), w=[kiA.k])
    sc.op("pool", lambda e: e.memset(kiB[:], 0.0), w=[kiB.k])
    sc.op("pool", lambda e: e.memset(qb[:, :, 192:256], 0.0), w=[qb.k])
    sc.op("pool", lambda e: e.memset(krb[:, 64:128], 0.0), w=[krb.k])
    sc.op("pool", lambda e: e.memset(QTr[64:128, :, :], 0.0), w=[QTr.k])

    posi = sb("posi", [128, NT], I32)
    sc.dma(posi[:], pos_d[:, :], w=[posi.k])
    sc.op("dve", lambda e: e.tensor_copy(out=posf[:], in_=posi[:]), r=[posi.k], w=[posf.k])
    invf = sb("invf", [128, 32], F32)
    sc.dma(invf[:], invf_d[:, :], w=[invf.k])
    _og32 = ogT.t[:].rearrange("p a b -> p (a b)").bitcast(F32)
    ang = T(_og32[:, 0:NT * 32].rearrange("p (t c) -> p t c", c=32), "ang")
    kk_ = T(_og32[:, 1024:1024 + NT * 32].rearrange("p (t c) -> p t c", c=32), "kk_")
    ang.k = ogT.k
    kk_.k = ogT.k
    for t in range(NT):
        sc.op("dve", lambda e, t=t: e.tensor_scalar(out=ang[:, t, :], in0=invf[:], scalar1=posf[:, t:t + 1],
                                                     scalar2=None, op0=MUL), r=[invf.k, posf.k], w=[ang.k])
    MAGIC = 12582912.0
    sc.op("dve", lambda e: e.tensor_scalar(out=kk_[:], in0=ang[:], scalar1=1.0 / TWO_PI, scalar2=MAGIC,
                                            op0=MUL, op1=ADD), r=[ang.k], w=[kk_.k])
    sc.op("dve", lambda e: e.tensor_scalar(out=kk_[:], in0=kk_[:], scalar1=MAGIC, scalar2=None, op0=SUB),
          r=[kk_.k], w=[kk_.k])
    sc.op("dve", lambda e: e.scalar_tensor_tensor(out=ang[:], in0=kk_[:], scalar=-CW1, in1=ang[:], op0=MUL, op1=ADD),
          r=[kk_.k, ang.k], w=[ang.k])
    sc.op("dve", lambda e: e.scalar_tensor_tensor(out=ang[:], in0=kk_[:], scalar=-CW2, in1=ang[:], op0=MUL, op1=ADD),
          r=[kk_.k, ang.k], w=[ang.k])
    sc.op("dve", lambda e: e.tensor_scalar(out=kk_[:], in0=ang[:], scalar1=PI / 2, scalar2=None, op0=ADD),
          r=[ang.k], w=[kk_.k])
    wrapm = cosT
    sc.op("dve", lambda e: e.tensor_scalar(out=wrapm[:], in0=kk_[:], scalar1=PI, scalar2=None, op0=ALU.is_gt),
          r=[kk_.k], w=[wrapm.k])
    sc.op("dve", lambda e: e.scalar_tensor_tensor(out=kk_[:], in0=wrapm[:], scalar=-TWO_PI, in1=kk_[:], op0=MUL, op1=ADD),
          r=[wrapm.k, kk_.k], w=[kk_.k])
    for tt_ in (ang, kk_):
        sc.op("dve", lambda e, tt_=tt_: e.tensor_scalar(out=tt_[:], in0=tt_[:], scalar1=PI, scalar2=-PI,
                                                         op0=ALU.min, op1=ALU.max), r=[tt_.k], w=[tt_.k])
    sc.op("act", lambda e: e.activation(out=sinT[:], in_=ang[:], func=AF.Sin), r=[ang.k], w=[sinT.k])
    sc.op("act", lambda e: e.activation(out=cosT[:], in_=kk_[:], func=AF.Sin), r=[kk_.k], w=[cosT.k])

    stg_i = [0]

    def conv_weight(dst_ap, src_ap, n, scale_ap, dst_tok, eng):
        s = stg[stg_i[0] % 2]
        stg_i[0] += 1
        sc.dma(s[0:src_ap.shape[0], 0:n], src_ap, w=[s.k])
        p = src_ap.shape[0]
        if scale_ap is None:
            sc.op(eng, lambda e: e.tensor_copy(out=dst_ap, in_=s[0:p, 0:n]), r=[s.k], w=[dst_tok])
        else:
            sc.op(eng, lambda e: e.tensor_scalar(out=dst_ap, in0=s[0:p, 0:n], scalar1=scale_ap, scalar2=1.0,
                                                 op0=MUL, op1=MUL), r=[s.k, vecs.k], w=[dst_tok])

    def gen_prep(l, which):
        if "win" in which:
            i = 0
            for kc in range(8):
                for q4 in range(4):
                    c0 = q4 * 628
                    conv_weight(Win[:, kc, c0:c0 + 628], w_in_d[l, kc * 128:(kc + 1) * 128, c0:c0 + 628], 628,
                                vecs[:, l, kc:kc + 1], Win.k, ("pool", "dve")[i % 2])
                    i += 1
                    yield
        if "small" in which:
            prep_small(l)
            yield
        if "wout" in which:
            i = 0
            for kc in range(8):
                sap = vecs[:, l, 10:11] if kc < 4 else None
                for hf in range(2):
                    conv_weight(Wout[:, kc, hf * 512:(hf + 1) * 512], w_out_d[l, kc * 128:(kc + 1) * 128, hf * 512:(hf + 1) * 512], 512,
                                sap, Wout.k, ("pool", "dve")[i % 2])
                    i += 1
                    yield

    def prep_win(l):
        i = 0
        for kc in range(8):
            for q4 in range(4):
                c0 = q4 * 628
                eng = ("pool", "dve")[i % 2]
                i += 1
                conv_weight(Win[:, kc, c0:c0 + 628], w_in_d[l, kc * 128:(kc + 1) * 128, c0:c0 + 628], 628,
                            vecs[:, l, kc:kc + 1], Win.k, eng)

    def prep_small(l):
        for kc in range(2):
            for hf in range(2):
                conv_weight(Wuq[:, kc, hf * 384:(hf + 1) * 384], w_uq_d[l, kc * 128:(kc + 1) * 128, hf * 384:(hf + 1) * 384], 384,
                            vecs[:, l, 11 + kc:12 + kc], Wuq.k, "pool")
        for hf in range(2):
            conv_weight(Wukv[:, hf * 512:(hf + 1) * 512], w_ukv_d[l, :, hf * 512:(hf + 1) * 512], 512, vecs[:, l, 13:14], Wukv.k, "pool")
        conv_weight(Wup[0:16, :], w_up_d[l, :, :], 256, None, Wup.k, "pool")
        sc.op("pool", lambda e: e.tensor_scalar(out=negb[:], in0=vecs[:, l, 8:10], scalar1=-1.0, scalar2=1.0,
                                                op0=MUL, op1=MUL), r=[vecs.k], w=[negb.k])
        sc.dma(qg[:], qg_d[l, :].partition_broadcast(128), w=[qg.k])
        sc.dma(kg[:], kg_d[l, :].partition_broadcast(128), w=[kg.k])

    def prep_wukt(l):
        for h in range(4):
            sc.op("pe", lambda e, h=h: e.transpose(ptr[:, h * 128:(h + 1) * 128], Wukv[:, h * 256:h * 256 + 128], ident[:]),
                  r=[Wukv.k, ident.k], w=[ptr.k], signal=(h == 3))
        sc.op("dve", lambda e: e.tensor_scalar(out=WukT[:], in0=ptr[:, 0:512].rearrange("p (h c) -> p h c", h=4),
                                               scalar1=vecs[:, l, 14:15], scalar2=None, op0=MUL),
              r=[ptr.k, vecs.k], w=[WukT.k])

    def prep_wout(l):
        i = 0
        for kc in range(8):
            sap = vecs[:, l, 10:11] if kc < 4 else None
            for hf in range(2):
                conv_weight(Wout[:, kc, hf * 512:(hf + 1) * 512], w_out_d[l, kc * 128:(kc + 1) * 128, hf * 512:(hf + 1) * 512], 512,
                            sap, Wout.k, ("pool", "dve")[i % 2])
                i += 1

    def rstd_from_ss(st, c0, c1, n_elems, extra_bias_ln=None):
        sc.op("act", lambda e: e.activation(out=st[:, c0:c1], in_=st[:, c0:c1], func=AF.Ln, scale=1.0 / n_elems, bias=EPS),
              r=[st.k], w=[st.k])
        if extra_bias_ln is None:
            sc.op("act", lambda e: e.activation(out=st[:, c0:c1], in_=st[:, c0:c1], func=AF.Exp, scale=-0.5),
                  r=[st.k], w=[st.k])
        else:
            sc.op("act", lambda e: e.activation(out=st[:, c0:c1], in_=st[:, c0:c1], func=AF.Exp, scale=-0.5,
                                                bias=extra_bias_ln), r=[st.k], w=[st.k])

    def transposes(srcs, dst_views, n_rows_list):
        pass

    class V_:
        def __init__(self, ap, tok):
            self.t, self.k = ap, tok

        def __getitem__(self, idx):
            return self.t[idx]

    G0, G1 = pst[0], pst[1]
    GT = V_(pO[0].t[:].rearrange("p a b -> p (a b)").bitcast(BF16), pO[0].k)
    PN = V_(pO[1].t[:].rearrange("p a b -> p (a b)").bitcast(BF16), pO[1].k)
    junkG = sb("junkG", [128, 256], BF16)
    Ocp = sb("Ocp", [128, 4, 129], F32)

    def stage_load_x(l, t, buf):
        sc.dma(buf[:], xbufs[l][t * 128:(t + 1) * 128, :], r=[xk(l, t)], w=[buf.k])

    def gen_norm(l, t):
        xb_ = xt[0]
        h_T = hT[t % 2]
        sc.op("act", lambda e: e.activation(out=hb[:], in_=xb_[:], func=AF.Square, accum_out=st_x[:, 0:1]),
              r=[xb_.k], w=[hb.k, st_x.k])
        yield SLP["n"]
        rstd_from_ss(st_x, 0, 1, D)
        yield SLP["n"]
        sc.op("dve", lambda e: e.tensor_scalar(out=hb[:], in0=xb_[:], scalar1=st_x[:, 0:1], scalar2=None, op0=MUL),
              r=[xb_.k, st_x.k], w=[hb.k])
        if t + 1 < NT:
            stage_load_x(l, t + 1, xt[0])
        yield SLP["n"]
        for kc in range(8):
            sc.op("pe", lambda e, kc=kc: e.transpose(PN[:, kc * 128:(kc + 1) * 128], hb[:, kc * 128:(kc + 1) * 128], ident[:]),
                  r=[hb.k, ident.k], w=[PN.k], signal=(kc == 7))
        yield SLP["n"]
        sc.op("dve", lambda e: e.tensor_copy(out=h_T[:], in_=PN[:, :].rearrange("p (k c) -> p k c", k=8)),
              r=[PN.k], w=[h_T.k])
        yield SLP["n"]
        PNf = pO[1].t[:].rearrange("p a b -> p (a b)")
        for kc in range(8):
            sc.op("pe", lambda e, kc=kc: e.matmul(PNf[:, 0:128], lhsT=h_T[:, kc, :], rhs=Win[:, kc, C_LR:C_LR + 128],
                                                  start=(kc == 0), stop=(kc == 7)),
                  r=[h_T.k, Win.k], w=[PN.k], signal=(kc == 7))
        yield SLP["n"]
        sc.op("act", lambda e: e.copy(out=lrb[:], in_=PNf[:, 0:128]), r=[PN.k], w=[lrb.k])
        yield SLP["n"]
        sc.op("pe", lambda e: e.transpose(PN[:, 512:640], lrb[:], ident[:]), r=[lrb.k, ident.k], w=[PN.k], signal=True)
        yield SLP["n"]
        sc.op("dve", lambda e: e.tensor_copy(out=glrT[t % 2][:], in_=PN[:, 512:640]), r=[PN.k], w=[glrT[t % 2].k])
        yield SLP["n"]

    def inproj_tok(h_T, c0, n, out_ap, out_tok):
        for kc in range(8):
            sc.op("pe", lambda e, kc=kc: e.matmul(out_ap, lhsT=h_T[:, kc, :], rhs=Win[:, kc, c0:c0 + n],
                                                  start=(kc == 0), stop=(kc == 7)),
                  r=[h_T.k, Win.k], w=[out_tok], signal=(kc == 7))

    def gen_silu_gate(src_ps, c0, tmp, dst_ap, dst_tok):
        sc.op("act", lambda e: e.activation(out=tmp[:], in_=src_ps[:, c0:c0 + 512], func=AF.Exp, scale=-1.0),
              r=[src_ps.k], w=[tmp.k])
        yield SLP["s"]
        sc.op("act", lambda e: e.activation(out=tmp[:], in_=tmp[:], func=AF.Ln, bias=1.0), r=[tmp.k], w=[tmp.k])
        yield SLP["s"]
        sc.op("act", lambda e: e.activation(out=tmp[:], in_=tmp[:], func=AF.Exp, scale=-1.0), r=[tmp.k], w=[tmp.k])
        yield SLP["s"]
        sc.op("dve", lambda e: e.tensor_tensor(out=dst_ap, in0=src_ps[:, c0:c0 + 512], in1=tmp[:], op=MUL),
              r=[src_ps.k, tmp.k], w=[dst_tok])
        yield

    GTf = V_(pO[0].t[:].rearrange("p a b -> p (a b)"), pO[0].k)

    def gen_gla_gates(l, t):
        for c2 in range(2):
            sc.op("pe", lambda e, c2=c2: e.matmul(GTf[:, 256 + c2 * 128:256 + (c2 + 1) * 128],
                                                  lhsT=Wup[:, c2 * 128:(c2 + 1) * 128], rhs=glrT[t % 2][:], start=True, stop=True),
                  r=[Wup.k, glrT[t % 2].k], w=[GTf.k], signal=(c2 == 1))
        yield SLP["a"]
        for c2 in range(2):
            sc.op("act", lambda e, c2=c2: e.activation(out=e1[:, c2, :], in_=GTf[:, 256 + c2 * 128:256 + (c2 + 1) * 128],
                                                       func=AF.Exp, scale=-1.0, bias=negb[:, c2:c2 + 1]),
                  r=[GTf.k, negb.k], w=[e1.k])
        yield SLP["a"]
        sc.op("act", lambda e: e.activation(out=e1[:], in_=e1[:], func=AF.Ln, bias=1.0), r=[e1.k], w=[e1.k])
        yield SLP["a"]
        for c2 in range(2):
            sc.op("dve", lambda e, c2=c2: e.tensor_tensor_scan(out=csb[:, c2, :], data0=ones128[:], data1=e1[:, c2, :],
                                                               initial=0.0, op0=MUL, op1=ADD),
                  r=[ones128.k, e1.k], w=[csb.k])
        yield SLP["a"]
        sc.op("act", lambda e: e.activation(out=Epl[:], in_=csb[:], func=AF.Exp, scale=-1.0 / 16.0), r=[csb.k], w=[Epl.k])
        sc.op("act", lambda e: e.activation(out=Emi[:], in_=csb[:], func=AF.Exp, scale=1.0 / 16.0), r=[csb.k], w=[Emi.k])
        yield SLP["a"]

    PNf32 = V_(pO[1].t[:].rearrange("p a b -> p (a b)"), pO[1].k)

    def gen_gate_g(l, t):
        h_T = hT[t % 2]
        inproj_tok(h_T, C_GG, 512, PNf32[:, 0:512], PNf32.k)
        yield 1
        yield from gen_silu_gate(PNf32, 0, sg_e, sgate[:], sgate.k)

    def gen_gate_m(l, t):
        h_T = hT[t % 2]
        ti = t % 4
        inproj_tok(h_T, C_MG, 512, PNf32[:, 0:512], PNf32.k)
        yield 1
        yield from gen_silu_gate(PNf32, 0, mg_e, smg[:, ti, :], smg.k)

    def gen_gates(l, t):
        yield from gen_gate_g(l, t)
        yield from gen_gate_m(l, t)

    def gen_norm_then_gates(l, t):
        if VARIANT != "glast":
            yield from gen_gate_g(l, t)
            if t + 1 < NT:
                yield from gen_norm(l, t + 1)
            yield from gen_gate_m(l, t)
        else:
            if t + 1 < NT:
                yield from gen_norm(l, t + 1)
            yield from gen_gates(l, t)

    def gen_gla(l, t):
        h_T = hT[t % 2]
        tc0 = (t % 4) * 128
        for c in range(4):
            col = (C_GQ if c < 2 else C_GK) + (c % 2) * 128
            for kc in range(8):
                sc.op("pe", lambda e, kc=kc, c=c, col=col: e.matmul(G0[:, c * 128:(c + 1) * 128],
                                                                     lhsT=Win[:, kc, col:col + 128], rhs=h_T[:, kc, :],
                                                                     start=(kc == 0), stop=(kc == 7)),
                      r=[h_T.k, Win.k], w=[G0.k], signal=(kc == 7 and c == 3))
            yield SLP["g"]
        inproj_tok(h_T, C_GV, 512, G1[:, 0:512], G1.k)
        yield 1
        sc.op("act", lambda e: e.copy(out=vb[:], in_=G1[:, 0:512]), r=[G1.k], w=[vb.k])
        yield SLP["g"]
        for hh in range(2):
            sc.op("dve", lambda e, hh=hh: e.scalar_tensor_tensor(out=qdm[:, :, hh, :],
                                                                 in0=G0[:, 0:256].rearrange("p (c t) -> p c t", c=2),
                                                                 scalar=hmask[:, hh:hh + 1], in1=Epl[:], op0=MUL, op1=MUL),
                  r=[G0.k, Epl.k, hmask.k], w=[qdm.k])
        sc.op("dve", lambda e: e.tensor_tensor(out=kiT[:], in0=G0[:, 256:512].rearrange("p (c t) -> p c t", c=2),
                                               in1=Emi[:], op=MUL), r=[G0.k, Emi.k], w=[kiT.k])
        yield SLP["g"]
        for c2 in range(2):
            sc.op("pe", lambda e, c2=c2: e.transpose(GT[:, c2 * 128:(c2 + 1) * 128], kiT[:, c2, :], ident[:]),
                  r=[kiT.k, ident.k], w=[GT.k], signal=(c2 == 1))
        for h in range(4):
            c2, hh = h // 2, h % 2
            sc.op("pe", lambda e, h=h, c2=c2, hh=hh: e.matmul(G0[:, h * 128:(h + 1) * 128], lhsT=kiT[:, c2, :],
                                                              rhs=qdm[:, c2, hh, :], start=True, stop=True),
                  r=[kiT.k, qdm.k], w=[G0.k], signal=(h == 3))
        yield SLP["g"]
        sc.op("dve", lambda e: e.tensor_copy(out=ki[:], in_=GT[:, 0:256].rearrange("p (c t) -> p c t", c=2)), r=[GT.k], w=[ki.k])
        sc.op("pool", lambda e: e.tensor_copy(out=kiA[:, :, 0:64], in_=ki[:, :, 0:64]), r=[ki.k], w=[kiA.k])
        sc.op("pool", lambda e: e.tensor_copy(out=kiB[:, :, 64:128], in_=ki[:, :, 64:128]), r=[ki.k], w=[kiB.k])
        sc.op("dve", lambda e: e.tensor_tensor(out=Am[:], in0=G0[:, 0:512].rearrange("p (h t) -> p h t", h=4),
                                               in1=triu4[:], op=MUL), r=[G0.k, triu4.k], w=[Am.k])
        yield SLP["g"]
        for h in range(4):
            c2, hh = h // 2, h % 2
            sc.op("pe", lambda e, h=h: e.matmul(G1[:, h * 128:(h + 1) * 128], lhsT=Am[:, h, :],
                                                rhs=vb[:, h * 128:(h + 1) * 128], start=True, stop=False),
                  r=[Am.k, vb.k], w=[G1.k], signal=False)
            sc.op("pe", lambda e, h=h, c2=c2, hh=hh: e.matmul(G1[:, h * 128:(h + 1) * 128], lhsT=qdm[:, c2, hh, :],
                                                              rhs=Sbf[:, c2, :], start=False, stop=True),
                  r=[qdm.k, Sbf.k], w=[G1.k], signal=(h == 3))
        for c2 in range(2):
            sc.op("pe", lambda e, c2=c2: e.matmul(G0[:, c2 * 128:(c2 + 1) * 128], lhsT=kiA[:, c2, :],
                                                  rhs=vb[:, (2 * c2) * 128:(2 * c2 + 1) * 128], start=True, stop=False),
                  r=[kiA.k, vb.k], w=[G0.k], signal=False)
            sc.op("pe", lambda e, c2=c2: e.matmul(G0[:, c2 * 128:(c2 + 1) * 128], lhsT=kiB[:, c2, :],
                                                  rhs=vb[:, (2 * c2 + 1) * 128:(2 * c2 + 2) * 128], start=False, stop=True),
                  r=[kiB.k, vb.k], w=[G0.k], signal=(c2 == 1))
        yield SLP["g"]
        for h in range(4):
            sc.op("act", lambda e, h=h: e.activation(out=junkG[:, 0:128], in_=G1[:, h * 128:(h + 1) * 128], func=AF.Square,
                                                     accum_out=st_g[:, h:h + 1]), r=[G1.k], w=[junkG.k, st_g.k])
        yield SLP["g"]
        sc.op("dve", lambda e: e.tensor_tensor(out=Sd[:], in0=G0[:, 0:256].rearrange("p (c t) -> p c t", c=2), in1=Sst[:], op=ADD),
              r=[G0.k, Sst.k], w=[Sd.k])
        yield SLP["g"]
        for c2 in range(2):
            sc.op("pool", lambda e, c2=c2: e.tensor_scalar(out=Sst[:, c2, :], in0=Sd[:, c2, :], scalar1=Epl[:, c2, 127:128],
                                                           scalar2=1.0, op0=MUL, op1=MUL), r=[Sd.k, Epl.k], w=[Sst.k])
        sc.op("pool", lambda e: e.tensor_copy(out=Sbf[:], in_=Sst[:]), r=[Sst.k], w=[Sbf.k])
        yield SLP["g"]
        rstd_from_ss(st_g, 0, 4, 128)
        yield SLP["g"]
        for h in range(4):
            sc.op("dve", lambda e, h=h: e.scalar_tensor_tensor(out=ogl[:, h, :], in0=G1[:, h * 128:(h + 1) * 128],
                                                               scalar=st_g[:, h:h + 1], in1=sgate[:, h * 128:(h + 1) * 128],
                                                               op0=MUL, op1=MUL), r=[G1.k, st_g.k, sgate.k], w=[ogl.k])
        yield SLP["g"]
        for h in range(4):
            sc.op("pe", lambda e, h=h: e.transpose(GT[:, h * 128:(h + 1) * 128], ogl[:, h, :], ident[:]),
                  r=[ogl.k, ident.k], w=[GT.k], signal=(h == 3))
        yield SLP["g"]
        sc.op("dve", lambda e: e.tensor_copy(out=ogT[:, 0:4, tc0:tc0 + 128],
                                             in_=GT[:, 0:512].rearrange("p (h c) -> p h c", h=4)),
              r=[GT.k], w=[ogT.k])
        yield SLP["g"]

    def gen_mla(l, t):
        h_T = hT[t % 2]
        ti = t % 4
        tc0 = ti * 128
        inproj_tok(h_T, C_CQ, 448, pmm[:, 0:448], pmm.k)
        yield 1
        sc.op("act", lambda e: e.activation(out=junk[:, 0:256], in_=pmm[:, 0:256], func=AF.Square, accum_out=st_c[:, 0:1]),
              r=[pmm.k], w=[junk.k, st_c.k])
        yield SLP["m"]
        sc.op("act", lambda e: e.activation(out=junk[:, 0:128], in_=pmm[:, 256:384], func=AF.Square, accum_out=st_c[:, 1:2]),
              r=[pmm.k], w=[junk.k, st_c.k])
        yield SLP["m"]
        sc.op("act", lambda e: e.activation(out=junk[:, 0:64], in_=pmm[:, 384:448], func=AF.Square, accum_out=st_c[:, 2:3]),
              r=[pmm.k], w=[junk.k, st_c.k])
        yield SLP["m"]
        rstd_from_ss(st_c, 0, 1, 256)
        yield SLP["m"]
        rstd_from_ss(st_c, 1, 2, 128)
        yield SLP["m"]
        sc.op("dve", lambda e: e.tensor_scalar(out=cn[:, 0:256], in0=pmm[:, 0:256], scalar1=st_c[:, 0:1], scalar2=None, op0=MUL),
              r=[pmm.k, st_c.k], w=[cn.k])
        sc.op("dve", lambda e: e.tensor_scalar(out=cn[:, 256:384], in0=pmm[:, 256:384], scalar1=st_c[:, 1:2], scalar2=None, op0=MUL),
              r=[pmm.k, st_c.k], w=[cn.k])
        sc.op("dve", lambda e: e.tensor_tensor(out=kpg[:], in0=pmm[:, 384:448], in1=kg[:, 128:192], op=MUL),
              r=[pmm.k, kg.k], w=[kpg.k])
        yield SLP["m"]
        for c in range(3):
            sc.op("pe", lambda e, c=c: e.transpose(ptr[:, c * 128:(c + 1) * 128], cn[:, c * 128:(c + 1) * 128], ident[:]),
                  r=[cn.k, ident.k], w=[ptr.k], signal=(c == 2))
        yield SLP["m"]
        sc.op("dve", lambda e: e.tensor_copy(out=cnT[:], in_=ptr[:, 0:384].rearrange("p (c t) -> p c t", c=3)),
              r=[ptr.k], w=[cnT.k])
        sc.op("pool", lambda e: e.tensor_copy(out=KT[:, t * 128:(t + 1) * 128], in_=cnT[:, 2, :]), r=[cnT.k], w=[KT.k])
        yield SLP["m"]
        c1_, s1_ = cosT[:, t, :], sinT[:, t, :]
        a1, a2 = kpg[:, 0:32], kpg[:, 32:64]
        r0_, r1_, r2_, r3_ = (rtmp[i][:, 0, :] for i in range(4))
        sc.op("pool", lambda e: e.tensor_tensor(out=r0_, in0=a1, in1=c1_, op=MUL), r=[kpg.k, cosT.k], w=[rtmp[0].k])
        sc.op("pool", lambda e: e.tensor_tensor(out=r1_, in0=a2, in1=s1_, op=MUL), r=[kpg.k, sinT.k], w=[rtmp[1].k])
        sc.op("pool", lambda e: e.tensor_tensor(out=krb[:, 0:32], in0=r0_, in1=r1_, op=SUB), r=[rtmp[0].k, rtmp[1].k], w=[krb.k])
        sc.op("pool", lambda e: e.tensor_tensor(out=r2_, in0=a1, in1=s1_, op=MUL), r=[kpg.k, sinT.k], w=[rtmp[2].k])
        sc.op("pool", lambda e: e.tensor_tensor(out=r3_, in0=a2, in1=c1_, op=MUL), r=[kpg.k, cosT.k], w=[rtmp[3].k])
        sc.op("pool", lambda e: e.tensor_tensor(out=krb[:, 32:64], in0=r2_, in1=r3_, op=ADD), r=[rtmp[2].k, rtmp[3].k], w=[krb.k])
        for (o0, n) in ((0, 512), (512, 256)):
            for kc in range(2):
                sc.op("pe", lambda e, kc=kc, o0=o0, n=n: e.matmul(pbig[:, o0:o0 + n], lhsT=cnT[:, kc, :],
                                                                  rhs=Wuq[:, kc, o0:o0 + n], start=(kc == 0), stop=(kc == 1)),
                      r=[cnT.k, Wuq.k], w=[pbig.k], signal=(kc == 1 and o0 == 512))
        yield 1
        for h in range(4):
            sc.op("act", lambda e, h=h: e.activation(out=junk[:, 0:192], in_=pbig[:, h * 192:(h + 1) * 192], func=AF.Square,
                                                     accum_out=st_q[:, h:h + 1]), r=[pbig.k], w=[junk.k, st_q.k])
            yield SLP["m"]
        rstd_from_ss(st_q, 0, 4, 192)
        yield SLP["m"]
        for h in range(4):
            sc.op("dve", lambda e, h=h: e.scalar_tensor_tensor(out=qn[:, h, :], in0=pbig[:, h * 192:(h + 1) * 192],
                                                               scalar=st_q[:, h:h + 1], in1=qg[:], op0=MUL, op1=MUL),
                  r=[pbig.k, st_q.k, qg.k], w=[qn.k])
            yield SLP["m"]
        for half in range(2):
            sc.op("pe", lambda e, half=half: e.matmul(pbig[:, half * 512:(half + 1) * 512], lhsT=cnT[:, 2, :],
                                                      rhs=Wukv[:, half * 512:(half + 1) * 512], start=True, stop=True),
                  r=[cnT.k, Wukv.k], w=[pbig.k], signal=(half == 1))
        sc.op("pe", lambda e: e.transpose(ptr[:, 512:640], krb[:], ident[:]), r=[krb.k, ident.k], w=[ptr.k], signal=True)
        yield SLP["m"]
        sc.op("dve", lambda e: e.tensor_copy(out=KTr[:, t * 128:(t + 1) * 128], in_=ptr[:, 512:640]), r=[ptr.k], w=[KTr.k])
        sc.op("pool", lambda e: e.tensor_copy(out=qb[:, :, 0:128], in_=qn[:, :, 0:128]), r=[qn.k], w=[qb.k])
        cos4 = cosT[:, t, :].unsqueeze(1).broadcast_to([128, 4, 32])
        sin4 = sinT[:, t, :].unsqueeze(1).broadcast_to([128, 4, 32])
        t1, t2 = qn[:, :, 128:160], qn[:, :, 160:192]
        sc.op("pool", lambda e: e.tensor_tensor(out=rtmp[0][:], in0=t1, in1=cos4, op=MUL), r=[qn.k, cosT.k], w=[rtmp[0].k])
        sc.op("pool", lambda e: e.tensor_tensor(out=rtmp[1][:], in0=t2, in1=sin4, op=MUL), r=[qn.k, sinT.k], w=[rtmp[1].k])
        yield SLP["m"]
        sc.op("pool", lambda e: e.tensor_tensor(out=qb[:, :, 128:160], in0=rtmp[0][:], in1=rtmp[1][:], op=SUB),
              r=[rtmp[0].k, rtmp[1].k], w=[qb.k])
        sc.op("pool", lambda e: e.tensor_tensor(out=rtmp[2][:], in0=t1, in1=sin4, op=MUL), r=[qn.k, sinT.k], w=[rtmp[2].k])
        yield SLP["m"]
        sc.op("pool", lambda e: e.tensor_tensor(out=rtmp[3][:], in0=t2, in1=cos4, op=MUL), r=[qn.k, cosT.k], w=[rtmp[3].k])
        sc.op("pool", lambda e: e.tensor_tensor(out=qb[:, :, 160:192], in0=rtmp[2][:], in1=rtmp[3][:], op=ADD),
              r=[rtmp[2].k, rtmp[3].k], w=[qb.k])
        yield SLP["m"]
        kv4 = pbig[:, :].rearrange("p (h c) -> p h c", h=4)
        for h in range(4):
            sc.op("act", lambda e, h=h: e.activation(out=junk[:, 0:128], in_=pbig[:, h * 256:h * 256 + 128], func=AF.Square,
                                                     accum_out=st_k[:, h:h + 1]), r=[pbig.k], w=[junk.k, st_k.k])
            yield SLP["m"]
        sc.op("act", lambda e: e.copy(out=Vc[:, t, :, 0:128], in_=kv4[:, :, 128:256]), r=[pbig.k], w=[Vc.k])
        sc.op("dve", lambda e: e.tensor_scalar(out=st_k[:, 0:4], in0=st_k[:, 0:4], scalar1=st_c[:, 2:3], scalar2=None, op0=ADD),
              r=[st_k.k, st_c.k], w=[st_k.k])
        yield SLP["m"]
        rstd_from_ss(st_k, 0, 4, 192, extra_bias_ln=float(np.log(192.0 ** -0.5)))
        yield SLP["m"]
        sc.op("pool", lambda e: e.tensor_copy(out=rks[:, t, :], in_=st_k[:, 0:4]), r=[st_k.k], w=[rks.k])
        for h in range(4):
            sc.op("pe", lambda e, h=h: e.transpose(ptr[:, h * 128:(h + 1) * 128], qb[:, h, 0:128], ident[:]),
                  r=[qb.k, ident.k], w=[ptr.k], signal=False)
        for h in range(4):
            sc.op("pe", lambda e, h=h: e.transpose(ptr[:, 512 + h * 128:512 + (h + 1) * 128], qb[:, h, 128:256], ident[:]),
                  r=[qb.k, ident.k], w=[ptr.k], signal=(h == 3))
        yield SLP["m"]
        sc.op("dve", lambda e: e.tensor_copy(out=QTn[:, :, tc0:tc0 + 128], in_=ptr[:, 0:512].rearrange("p (h c) -> p h c", h=4)),
              r=[ptr.k], w=[QTn.k])
        sc.op("dve", lambda e: e.tensor_copy(out=QTr[:, :, tc0:tc0 + 128],
                                             in_=ptr[:, 512:1024].rearrange("p (h c) -> p h c", h=4)),
              r=[ptr.k], w=[QTr.k])
        yield SLP["m"]

    step_ctr = [0]

    def run_interleaved(gens):
        gens = [g for g in gens if g is not None]
        sleep = {}
        lim = int(VARIANT.split(":")[1]) if VARIANT.startswith("steps:") else None
        while gens:
            progressed = False
            for g in list(gens):
                if sleep.get(id(g), 0) > 0 and len(gens) > 1:
                    sleep[id(g)] -= 1
                    continue
                if lim is not None and step_ctr[0] >= lim:
                    raise _Stop()
                step_ctr[0] += 1
                progressed = True
                try:
                    res = next(g)
                    if isinstance(res, tuple) and res[0] == "spawn":
                        gens.append(res[1])
                    elif isinstance(res, int) and PIPE_DIST:
                        sleep[id(g)] = res * PIPE_DIST
                except StopIteration:
                    while g in gens:
                        gens.remove(g)
            if not progressed:
                for k in sleep:
                    sleep[k] = 0

    pt_i = [0]
    ps_i = [0]

    def stage_attention(l, b, bg=None):
        outs = [(pst[0], pst[0][:, 0:512]), (pst[1], pst[1][:, 0:512]), (pmm, pmm[:, 0:512]), (pbig, pbig[:, 0:512])]
        for h in range(4):
            tk, ap = outs[h]
            sc.op("pe", lambda e, h=h, ap=ap: e.matmul(ap, lhsT=WukT[:, h, :], rhs=QTn[:, h, :], start=True, stop=True),
                  r=[WukT.k, QTn.k], w=[tk.k], signal=True)
        for h in range(4):
            tk, ap = outs[h]
            sc.op(("dve", "act")[h % 2], lambda e, h=h, ap=ap: (e.tensor_copy(out=QTn[:, h, :], in_=ap) if h % 2 == 0
                                                                 else e.copy(out=QTn[:, h, :], in_=ap)),
                  r=[tk.k], w=[QTn.k])
        nk = 4 * b + 4
        steps = [(h, j) for h in range(4) for j in range(nk)]
        bufs = {}

        def emit_st(i):
            h, j = steps[i]
            r = j - 4 * b
            q0 = 128 * r if r > 0 else 0
            n = 512 - q0
            pst_ = pst3[ps_i[0] % 3]
            ps_i[0] += 1
            bufs[i] = pst_
            sc.op("pe", lambda e: e.matmul(pst_[:, 0:n], lhsT=KT[:, j * 128:(j + 1) * 128], rhs=QTn[:, h, q0:512],
                                           start=True, stop=False), r=[KT.k, QTn.k], w=[pst_.k], signal=False)
            sc.op("pe", lambda e: e.matmul(pst_[:, 0:n], lhsT=KTr[:, j * 128:(j + 1) * 128], rhs=QTr[:, h, q0:512],
                                           start=False, stop=True), r=[KTr.k, QTr.k], w=[pst_.k], signal=True)

        first = [True, True]
        pst3 = [pst[0], pst[1], pmm]
        emit_st(0)
        if len(steps) > 1:
            emit_st(1)
        for i, (h, j) in enumerate(steps):
            if bg is not None and i % 2 == 1:
                next(bg, None)
            if i + 2 < len(steps):
                emit_st(i + 2)
            r = j - 4 * b
            q0 = 128 * r if r > 0 else 0
            n = 512 - q0
            pst_ = bufs.pop(i)
            PT_ = PT[pt_i[0] % 4]
            pt_i[0] += 1
            sc.op("act", lambda e: e.activation(out=PT_[:, 0:n], in_=pst_[:, 0:n], func=AF.Exp, scale=rks[:, j, h:h + 1]),
                  r=[pst_.k, rks.k], w=[PT_.k])
            if r >= 0:
                sc.op("pool", lambda e: e.tensor_tensor(out=PT_[:, 0:128], in0=PT_[:, 0:128], in1=triu4[:, 0, :], op=MUL),
                      r=[PT_.k, triu4.k], w=[PT_.k])
            if j == 0:
                first = [True, True]
            for s in range(max(r, 0), 4):
                bank = s // 2
                st_flag = first[bank]
                first[bank] = False
                last = (j == 4 * b + s)
                c0 = 128 * s - q0
                sc.op("pe", lambda e, s=s, bank=bank, st_flag=st_flag, last=last, c0=c0: e.matmul(
                    pO[bank][:, s % 2, 0:129], lhsT=PT_[:, c0:c0 + 128], rhs=Vc[:, j, h, :],
                    start=st_flag, stop=last, skip_group_check=True),
                    r=[PT_.k, Vc.k], w=[pO[bank].k], signal=(s == 3))
            if j == nk - 1:
                for bank in range(2):
                    sc.op("dve", lambda e, bank=bank: e.tensor_copy(out=Ocp[:, 2 * bank:2 * bank + 2, :],
                                                                    in_=pO[bank][:, :, 0:129]),
                          r=[pO[bank].k], w=[Ocp.k])
                sc.op("dve", lambda e: e.reciprocal(out=rinv[:, 0:4], in_=Ocp[:, :, 128]), r=[Ocp.k], w=[rinv.k])
                for s in range(4):
                    sc.op("dve", lambda e, s=s: e.scalar_tensor_tensor(out=om[:, s, :], in0=Ocp[:, s, 0:128],
                                                                       scalar=rinv[:, s:s + 1],
                                                                       in1=smg[:, s, h * 128:(h + 1) * 128], op0=MUL, op1=MUL),
                          r=[Ocp.k, rinv.k, smg.k], w=[om.k])
                for s in range(4):
                    sc.op("pe", lambda e, s=s: e.transpose(ptr[:, s * 128:(s + 1) * 128], om[:, s, :], ident[:]),
                          r=[om.k, ident.k], w=[ptr.k], signal=(s == 3))
                sc.op("dve", lambda e: e.tensor_copy(out=ogT[:, 4 + h, :], in_=ptr[:, 0:512]), r=[ptr.k], w=[ogT.k])

    def xr_load(l, t):
        xr_ = xr[t % 2]
        sc.dma(xr_[:], xbufs[l][t * 128:(t + 1) * 128, :], r=[xk(l, t)], w=[xr_.k])

    def stage_outproj(l, b):
        for s in range(4):
            t = 4 * b + s
            xr_ = xr[t % 2]
            for half in range(2):
                for kc in range(8):
                    sc.op("pe", lambda e, half=half, kc=kc: e.matmul(pbig[:, half * 512:(half + 1) * 512],
                                                                     lhsT=ogT[:, kc, s * 128:(s + 1) * 128],
                                                                     rhs=Wout[:, kc, half * 512:(half + 1) * 512],
                                                                     start=(kc == 0), stop=(kc == 7)),
                          r=[ogT.k, Wout.k], w=[pbig.k], signal=(kc == 7 and half == 1))
            sc.op("dve", lambda e: e.tensor_tensor(out=xr_[:], in0=pbig[:, :], in1=xr_[:], op=ADD),
                  r=[pbig.k, xr_.k], w=[xr_.k])
            sc.dma(xbufs[l + 1][t * 128:(t + 1) * 128, :], xr_[:], r=[xr_.k], w=[xk(l + 1, t)])
            if s + 2 < 4:
                xr_load(l, t + 2)

    def chk(name):
        if stop_after == name:
            raise _Stop()

    try:
        pending_wout = [None]
        chk("init")
        prep_win(0)
        prep_small(0)
        prep_wukt(0)
        prep_wout(0)
        chk("prep")
        for l in range(depth):
            sc.op("pool", lambda e: e.memset(Sst[:], 0.0), w=[Sst.k])
            sc.op("pool", lambda e: e.memset(Sbf[:], 0.0), w=[Sbf.k])
            stage_load_x(l, 0, xt[0])
            run_interleaved([gen_norm(l, 0)])
            chk("norm0")
            for t in range(NT):
                gm = gen_mla(l, t)
                extra = pending_wout[0]
                pending_wout[0] = None
                if VARIANT == "mla2":
                    run_interleaved([gm, gen_gla_gates(l, t), gen_gla(l, t), gm, gen_norm_then_gates(l, t)])
                else:
                    chains = {"a": gen_gla_gates(l, t), "g": gen_gla(l, t), "m": gm, "n": gen_norm_then_gates(l, t)}
                    order = VARIANT[4:] if VARIANT.startswith("ord:") else "agmn"
                    run_interleaved([chains[c] for c in order] + [extra])
                chk("mla")
                bg = None
                if t == NT - 1 and l + 1 < depth:
                    bg = gen_prep(l + 1, ("win", "small"))
                if t % 4 == 3:
                    b = t // 4
                    xr_load(l, 4 * b)
                    xr_load(l, 4 * b + 1)
                    if VARIANT != "noattn":
                        stage_attention(l, b, bg)
                    if bg is not None:
                        for _ in bg:
                            pass
                    chk("attn")
                    stage_outproj(l, b)
                    chk("out")
            if l + 1 < depth:
                prep_wukt(l + 1)
                pending_wout[0] = gen_prep(l + 1, ("wout",))
    except _Stop:
        pass
    for en in ("pe", "act", "dve", "pool"):
        e_ = sc.eng[en]
        if e_["n"] > 0:
            sc._wait("sp", Ev(en, e_["sem"], e_["n"]))
    sc.wait_all_dma("sp")
    nc._sched_ninst = sc.ninst
    return nc


def _host_inputs(inputs, S, depth):
    f32 = np.float32
    vecs = np.zeros((depth, 128, NVEC), f32)
    vecs[:, :, 0:8] = inputs["norm_g"][:depth].reshape(depth, 8, 128).transpose(0, 2, 1)
    vecs[:, :, 8:10] = inputs["b_gla_gate"][:depth].reshape(depth, 2, 128).transpose(0, 2, 1)
    vecs[:, :, 10] = inputs["gla_norm_g"][:depth]
    vecs[:, :, 11:13] = inputs["mla_q_norm_g"][:depth].reshape(depth, 2, 128).transpose(0, 2, 1)
    vecs[:, :, 13] = inputs["mla_kv_norm_g"][:depth]
    vecs[:, :, 14] = inputs["k_head_g"][:depth, 0:128]
    invf = (10000.0 ** (-np.arange(0, 64, 2, dtype=f32) / f32(64))).astype(f32)
    common = {
        "invf": np.ascontiguousarray(np.broadcast_to(invf[None, :], (128, 32))).astype(f32),
        "ident": np.eye(128, dtype=f32),
        "triu": np.triu(np.ones((128, 128), f32)),
        "w_in": np.ascontiguousarray(np.concatenate([inputs["w_in"][:depth, :, 0:1024], inputs["w_in"][:depth, :, 1040:1552],
                                                     inputs["w_in"][:depth, :, 1024:1040], inputs["w_in"][:depth, :, 1552:]], axis=2), dtype=f32),
        "w_up": np.ascontiguousarray(inputs["w_gla_gate_up"][:depth], dtype=f32),
        "vecs": vecs,
        "w_uq": np.ascontiguousarray(inputs["w_uq"][:depth], dtype=f32),
        "w_ukv": np.ascontiguousarray(inputs["w_ukv"][:depth], dtype=f32),
        "q_head_g": np.ascontiguousarray(inputs["q_head_g"][:depth], dtype=f32),
        "k_head_g": np.ascontiguousarray(inputs["k_head_g"][:depth], dtype=f32),
        "w_out": np.ascontiguousarray(inputs["w_out"][:depth], dtype=f32),
    }
    NT = S // 128
    in_maps = []
    for c in range(8):
        b = c % BATCH
        m = dict(common)
        m["x"] = np.ascontiguousarray(inputs["x"][b, :S], dtype=f32)
        m["pos"] = np.ascontiguousarray(inputs["positions"][b, :S].reshape(NT, 128).T.astype(np.int32))
        in_maps.append(m)
    return in_maps


_NC_CACHE = {}


def run(inputs, S=SEQ, depth=DEPTH):
    key = (S, depth)
    if key not in _NC_CACHE:
        _NC_CACHE[key] = build_program(S, depth)
    nc = _NC_CACHE[key]
    in_maps = _host_inputs(inputs, S, depth)
    res = run_bass_kernel_spmd(nc, in_maps, core_ids=list(range(8)))
    return np.stack([res.results[b]["y"] for b in range(BATCH)], axis=0)


def kernel(**inputs):
    inputs = {k: np.asarray(v) for k, v in inputs.items()}
    return run(inputs).astype(np.float32)
```

```python
import numpy as np
import concourse.bass as bass
import concourse.mybir as mybir
from concourse.bass_utils import run_bass_kernel_spmd

F32 = mybir.dt.float32
BF16 = mybir.dt.bfloat16
I32 = mybir.dt.int32
AF = mybir.ActivationFunctionType
ALU = mybir.AluOpType

D = 1024
DIN = 2512
DEPTH = 4
SEQ = 4096
BATCH = 4
EPS = 1e-6
C_GQ, C_GK, C_GV, C_GG, C_LR, C_CQ, C_CKV, C_KPE, C_MG = 0, 256, 512, 1024, 1536, 1552, 1808, 1936, 2000
TWO_PI = 6.283185307179586
CW1 = 6.28125
CW2 = TWO_PI - CW1
PI = 3.141592653589793
NVEC = 16
VARIANT = ""
SKIP_OLD_SELF = False
PIPE_DIST = 1
SLEEP_DEFAULT = "sn"


class Ev:
    __slots__ = ("key", "sem", "val")

    def __init__(self, key, sem, val):
        self.key, self.sem, self.val = key, sem, val


class Tok:
    __slots__ = ("name", "w", "rs")

    def __init__(self, name):
        self.name, self.w, self.rs = name, None, []


class Sched:
    def __init__(self, nc, n_dma_sems=14):
        self.nc = nc
        self.eng = {}
        for name, h in (("pe", nc.tensor), ("act", nc.scalar), ("dve", nc.vector),
                        ("pool", nc.gpsimd), ("sp", nc.sync)):
            self.eng[name] = dict(h=h, sem=nc.alloc_semaphore(name=f"s_{name}"), n=0, seen={}, pending=[])
        self.dsems = [nc.alloc_semaphore(name=f"s_dma{i}") for i in range(n_dma_sems)]
        self.dcount = [0] * n_dma_sems
        self.dnext = 0
        self.ninst = 0

    def _wait(self, ename, ev):
        e = self.eng[ename]
        if ev.key == "pe" and ename == "pe":
            return
        if ev.val is None:
            raise RuntimeError(f"wait on unresolved event ({ev.key}) from {ename}")
        if e["seen"].get(ev.key, 0) >= ev.val:
            return
        if SKIP_OLD_SELF and ev.key == ename and ev.val <= e["n"] - 2:
            return
        e["h"].wait_ge(ev.sem, ev.val)
        e["seen"][ev.key] = ev.val

    def _deps(self, ename, r, w):
        for t in r:
            if t.w is not None:
                self._wait(ename, t.w)
        for t in w:
            if t.w is not None:
                self._wait(ename, t.w)
            for ev in t.rs:
                self._wait(ename, ev)

    def _update(self, ev, r, w):
        for t in r:
            t.rs.append(ev)
        for t in w:
            t.w = ev
            t.rs = []

    def op(self, ename, fn, r=(), w=(), signal=True):
        e = self.eng[ename]
        self._deps(ename, r, w)
        ins = fn(e["h"])
        self.ninst += 1
        ev = Ev(ename, e["sem"], None)
        if signal:
            e["n"] += 1
            ins.then_inc(e["sem"], 1)
            ev.val = e["n"]
            for p in e["pending"]:
                p.val = e["n"]
            e["pending"] = []
        else:
            e["pending"].append(ev)
        self._update(ev, r, w)
        return ins

    def dma(self, out, in_, r=(), w=(), q="sp"):
        e = self.eng[q]
        i = self.dnext
        self.dnext = (self.dnext + 1) % len(self.dsems)
        sem = self.dsems[i]
        if self.dcount[i] > 0:
            self._wait(q, Ev(f"d{i}", sem, 16 * self.dcount[i]))
        self._deps(q, r, w)
        ins = e["h"].dma_start(out=out, in_=in_)
        self.ninst += 1
        self.dcount[i] += 1
        ins.then_inc(sem, 16)
        ev = Ev(f"d{i}", sem, 16 * self.dcount[i])
        self._update(ev, r, w)
        return ev

    def wait_all_dma(self, q="sp"):
        for i, sem in enumerate(self.dsems):
            if self.dcount[i] > 0:
                self._wait(q, Ev(f"d{i}", sem, 16 * self.dcount[i]))


class T:
    def __init__(self, ap_src, name):
        self.t = ap_src
        self.k = Tok(name)

    def __getitem__(self, idx):
        return self.t[idx]


def build_program(S=SEQ, depth=DEPTH, stop_after=None):
    NT = S // 128
    NBLK = S // 512
    assert S % 512 == 0
    nc = bass.Bass("TRN2", target_bir_lowering=False)
    sc = Sched(nc)

    class _Stop(Exception):
        pass

    SLP = {k: (1 if (VARIANT.startswith("sl:") and k in VARIANT[3:]) or k in SLEEP_DEFAULT else None) for k in "agmns"}

    def dram(name, shape, dt, kind):
        return nc.dram_tensor(name, list(shape), dt, kind=kind).ap()

    x_in = dram("x", [S, D], F32, "ExternalInput")
    pos_d = dram("pos", [128, NT], I32, "ExternalInput")
    invf_d = dram("invf", [128, 32], F32, "ExternalInput")
    ident_d = dram("ident", [128, 128], F32, "ExternalInput")
    triu_d = dram("triu", [128, 128], F32, "ExternalInput")
    w_in_d = dram("w_in", [depth, D, DIN], F32, "ExternalInput")
    w_up_d = dram("w_up", [depth, 16, 256], F32, "ExternalInput")
    vecs_d = dram("vecs", [depth, 128, NVEC], F32, "ExternalInput")
    w_uq_d = dram("w_uq", [depth, 256, 768], F32, "ExternalInput")
    w_ukv_d = dram("w_ukv", [depth, 128, 1024], F32, "ExternalInput")
    qg_d = dram("q_head_g", [depth, 192], F32, "ExternalInput")
    kg_d = dram("k_head_g", [depth, 192], F32, "ExternalInput")
    w_out_d = dram("w_out", [depth, D, D], F32, "ExternalInput")
    y_out = dram("y", [S, D], F32, "ExternalOutput")
    scr = [dram("xs0", [S, D], F32, "Internal"), dram("xs1", [S, D], F32, "Internal")]

    xbufs = [x_in] + [scr[l % 2] for l in range(depth - 1)] + [y_out]
    xtok = {}

    def xk(buf_idx, t):
        if buf_idx == 0:
            key = ("in", t)
        elif buf_idx == depth:
            key = ("out", t)
        else:
            key = ("s", (buf_idx - 1) % 2, t)
        if key not in xtok:
            xtok[key] = Tok(f"x{key}")
        return xtok[key]

    def sb(name, shape, dt):
        return T(nc.alloc_sbuf_tensor("sb_" + name, list(shape), dt), name)

    def ps(name, shape, dt):
        return T(nc.alloc_psum_tensor("ps_" + name, list(shape), dt), name)

    Win = sb("Win", [128, 8, DIN], BF16)
    Wout = sb("Wout", [128, 8, D], BF16)
    Wuq = sb("Wuq", [128, 2, 768], BF16)
    Wukv = sb("Wukv", [128, 1024], BF16)
    Wup = sb("Wup", [128, 256], BF16)
    stg = [sb(f"stg{i}", [128, 628], F32) for i in range(2)]
    KT = sb("KT", [128, S], BF16)
    WukT = sb("WukT", [128, 4, 128], BF16)
    KTr = sb("KTr", [128, S], BF16)
    Vc = sb("Vc", [128, NT, 4, 129], BF16)
    rks = sb("rks", [128, NT, 4], F32)
    cosT = sb("cosT", [128, NT, 32], F32)
    sinT = sb("sinT", [128, NT, 32], F32)
    ident = sb("ident", [128, 128], BF16)
    triu4 = sb("triu4", [128, 4, 128], BF16)
    vecs = sb("vecs", [128, depth, NVEC], F32)
    negb = sb("negb", [128, 2], F32)
    qg = sb("qg", [128, 192], F32)
    kg = sb("kg", [128, 192], F32)
    xt = [sb(f"xt{i}", [128, D], F32) for i in range(1)]
    xr = [sb(f"xr{i}", [128, D], F32) for i in range(2)]
    hb = sb("hb", [128, D], BF16)
    hT = [sb(f"hT{i}", [128, 8, 128], BF16) for i in range(2)]
    junk = sb("junk", [128, 256], BF16)
    st_x = sb("st_x", [128, 4], F32)
    glrT = [sb(f"glrT{i}", [128, 128], BF16) for i in range(2)]
    lrb = sb("lrb", [128, 128], BF16)
    e1 = sb("e1", [128, 2, 128], F32)
    csb = sb("csb", [128, 2, 128], F32)
    ones128 = sb("ones128", [128, 128], F32)
    Epl = sb("Epl", [128, 2, 128], F32)
    Emi = sb("Emi", [128, 2, 128], F32)
    qdm = sb("qdm", [128, 2, 2, 128], BF16)
    hmask = sb("hmask", [128, 2], F32)
    kiT = sb("kiT", [128, 2, 128], BF16)
    ki = sb("ki", [128, 2, 128], BF16)
    kiA = sb("kiA", [128, 2, 128], BF16)
    kiB = sb("kiB", [128, 2, 128], BF16)
    vb = sb("vb", [128, 512], BF16)
    Am = sb("Am", [128, 4, 128], BF16)
    Sst = sb("Sst", [128, 2, 128], F32)
    Sd = sb("Sd", [128, 2, 128], F32)
    Sbf = sb("Sbf", [128, 2, 128], BF16)
    sg_e = sb("sg_e", [128, 512], F32)
    sgate = sb("sgate", [128, 512], BF16)
    st_g = sb("st_g", [128, 8], F32)
    ogl = sb("ogl", [128, 4, 128], BF16)
    st_c = sb("st_c", [128, 8], F32)
    cn = sb("cn", [128, 384], BF16)
    cnT = sb("cnT", [128, 3, 128], BF16)
    kpg = sb("kpg", [128, 64], F32)
    qn = sb("qn", [128, 4, 192], F32)
    qb = sb("qb", [128, 4, 256], BF16)
    rtmp = [sb(f"rtmp{i}", [128, 4, 32], F32) for i in range(4)]
    st_q = sb("st_q", [128, 8], F32)
    st_k = sb("st_k", [128, 8], F32)
    krb = sb("krb", [128, 128], BF16)
    QTn = sb("QTn", [128, 4, 512], BF16)
    QTr = sb("QTr", [128, 4, 512], BF16)
    smg = sb("smg", [128, 4, 512], BF16)
    mg_e = sb("mg_e", [128, 512], F32)
    PT = [sb(f"PT{i}", [128, 512], BF16) for i in range(4)]
    rinv = sb("rinv", [128, 4], F32)
    om = sb("om", [128, 4, 128], BF16)
    ogT = sb("ogT", [128, 8, 512], BF16)
    posf = sb("posf", [128, NT], F32)

    pbig = ps("pbig", [128, 1024], F32)
    pmm = ps("pmm", [128, 512], F32)
    ptr = ps("ptr", [128, 1024], BF16)
    pst = [ps(f"pst{i}", [128, 512], F32) for i in range(2)]
    pO = [ps(f"pO{i}", [128, 2, 256], F32) for i in range(2)]

    MUL, ADD, SUB = ALU.mult, ALU.add, ALU.subtract

    def load_const_bf16(dst_ap_fn, src_ap, n, dst_tok):
        sc.dma(stg[0][:, 0:n], src_ap, w=[stg[0].k])
        sc.op("dve", lambda e: e.tensor_copy(out=dst_ap_fn(), in_=stg[0][:, 0:n]), r=[stg[0].k], w=[dst_tok])

    load_const_bf16(lambda: ident[:], ident_d[:, :], 128, ident.k)
    sc.dma(stg[1][:, 0:128], triu_d[:, :], w=[stg[1].k])
    for h in range(4):
        sc.op("dve", lambda e, h=h: e.tensor_copy(out=triu4[:, h, :], in_=stg[1][:, 0:128]), r=[stg[1].k], w=[triu4.k])
    sc.dma(vecs[:], vecs_d.rearrange("l p v -> p l v"), w=[vecs.k])
    sc.op("pool", lambda e: e.memset(ones128[:], 1.0), w=[ones128.k])
    sc.op("pool", lambda e: e.memset(hmask[0:64, 0:1], 0.125), w=[hmask.k])
    sc.op("pool", lambda e: e.memset(hmask[64:128, 0:1], 0.0), w=[hmask.k])
    sc.op("pool", lambda e: e.memset(hmask[0:64, 1:2], 0.0), w=[hmask.k])
    sc.op("pool", lambda e: e.memset(hmask[64:128, 1:2], 0.125), w=[hmask.k])
    sc.op("pool", lambda e: e.memset(Vc[:, :, :, 128:129], 1.0), w=[Vc.k])
    sc.op("pool", lambda e: e.memset(KTr[64:128, :], 0.0), w=[KTr.k])
    sc.op("pool", lambda e: e.memset(Wup[:], 0.0), w=[Wup.k])
    sc.op("pool", lambda e: e.memset(kiA[:], 0.0), w=[kiA.k])
    sc.op("pool", lambda e: e.memset(kiB[:], 0.0), w=[kiB.k])
    sc.op("pool", lambda e: e.memset(qb[:, :, 192:256], 0.0), w=[qb.k])
    sc.op("pool", lambda e: e.memset(krb[:, 64:128], 0.0), w=[krb.k])
    sc.op("pool", lambda e: e.memset(QTr[64:128, :, :], 0.0), w=[QTr.k])

    posi = sb("posi", [128, NT], I32)
    sc.dma(posi[:], pos_d[:, :], w=[posi.k])
    sc.op("dve", lambda e: e.tensor_copy(out=posf[:], in_=posi[:]), r=[posi.k], w=[posf.k])
    invf = sb("invf", [128, 32], F32)
    sc.dma(invf[:], invf_d[:, :], w=[invf.k])
    _og32 = ogT.t[:].rearrange("p a b -> p (a b)").bitcast(F32)
    ang = T(_og32[:, 0:NT * 32].rearrange("p (t c) -> p t c", c=32), "ang")
    kk_ = T(_og32[:, 1024:1024 + NT * 32].rearrange("p (t c) -> p t c", c=32), "kk_")
    ang.k = ogT.k
    kk_.k = ogT.k
    for t in range(NT):
        sc.op("dve", lambda e, t=t: e.tensor_scalar(out=ang[:, t, :], in0=invf[:], scalar1=posf[:, t:t + 1],
                                                     scalar2=None, op0=MUL), r=[invf.k, posf.k], w=[ang.k])
    MAGIC = 12582912.0
    sc.op("dve", lambda e: e.tensor_scalar(out=kk_[:], in0=ang[:], scalar1=1.0 / TWO_PI, scalar2=MAGIC,
                                            op0=MUL, op1=ADD), r=[ang.k], w=[kk_.k])
    sc.op("dve", lambda e: e.tensor_scalar(out=kk_[:], in0=kk_[:], scalar1=MAGIC, scalar2=None, op0=SUB),
          r=[kk_.k], w=[kk_.k])
    sc.op("dve", lambda e: e.scalar_tensor_tensor(out=ang[:], in0=kk_[:], scalar=-CW1, in1=ang[:], op0=MUL, op1=ADD),
          r=[kk_.k, ang.k], w=[ang.k])
    sc.op("dve", lambda e: e.scalar_tensor_tensor(out=ang[:], in0=kk_[:], scalar=-CW2, in1=ang[:], op0=MUL, op1=ADD),
          r=[kk_.k, ang.k], w=[ang.k])
    sc.op("dve", lambda e: e.tensor_scalar(out=kk_[:], in0=ang[:], scalar1=PI / 2, scalar2=None, op0=ADD),
          r=[ang.k], w=[kk_.k])
    wrapm = cosT
    sc.op("dve", lambda e: e.tensor_scalar(out=wrapm[:], in0=kk_[:], scalar1=PI, scalar2=None, op0=ALU.is_gt),
          r=[kk_.k], w=[wrapm.k])
    sc.op("dve", lambda e: e.scalar_tensor_tensor(out=kk_[:], in0=wrapm[:], scalar=-TWO_PI, in1=kk_[:], op0=MUL, op1=ADD),
          r=[wrapm.k, kk_.k], w=[kk_.k])
    for tt_ in (ang, kk_):
        sc.op("dve", lambda e, tt_=tt_: e.tensor_scalar(out=tt_[:], in0=tt_[:], scalar1=PI, scalar2=-PI,
                                                         op0=ALU.min, op1=ALU.max), r=[tt_.k], w=[tt_.k])
    sc.op("act", lambda e: e.activation(out=sinT[:], in_=ang[:], func=AF.Sin), r=[ang.k], w=[sinT.k])
    sc.op("act", lambda e: e.activation(out=cosT[:], in_=kk_[:], func=AF.Sin), r=[kk_.k], w=[cosT.k])

    stg_i = [0]

    def conv_weight(dst_ap, src_ap, n, scale_ap, dst_tok, eng):
        s = stg[stg_i[0] % 2]
        stg_i[0] += 1
        sc.dma(s[0:src_ap.shape[0], 0:n], src_ap, w=[s.k])
        p = src_ap.shape[0]
        if scale_ap is None:
            sc.op(eng, lambda e: e.tensor_copy(out=dst_ap, in_=s[0:p, 0:n]), r=[s.k], w=[dst_tok])
        else:
            sc.op(eng, lambda e: e.tensor_scalar(out=dst_ap, in0=s[0:p, 0:n], scalar1=scale_ap, scalar2=1.0,
                                                 op0=MUL, op1=MUL), r=[s.k, vecs.k], w=[dst_tok])

    def gen_prep(l, which):
        if "win" in which:
            i = 0
            for kc in range(8):
                for q4 in range(4):
                    c0 = q4 * 628
                    conv_weight(Win[:, kc, c0:c0 + 628], w_in_d[l, kc * 128:(kc + 1) * 128, c0:c0 + 628], 628,
                                vecs[:, l, kc:kc + 1], Win.k, ("pool", "dve")[i % 2])
                    i += 1
                    yield
        if "small" in which:
            prep_small(l)
            yield
        if "wout" in which:
            i = 0
            for kc in range(8):
                sap = vecs[:, l, 10:11] if kc < 4 else None
                for hf in range(2):
                    conv_weight(Wout[:, kc, hf * 512:(hf + 1) * 512], w_out_d[l, kc * 128:(kc + 1) * 128, hf * 512:(hf + 1) * 512], 512,
                                sap, Wout.k, ("pool", "dve")[i % 2])
                    i += 1
                    yield

    def prep_win(l):
        i = 0
        for kc in range(8):
            for q4 in range(4):
                c0 = q4 * 628
                eng = ("pool", "dve")[i % 2]
                i += 1
                conv_weight(Win[:, kc, c0:c0 + 628], w_in_d[l, kc * 128:(kc + 1) * 128, c0:c0 + 628], 628,
                            vecs[:, l, kc:kc + 1], Win.k, eng)

    def prep_small(l):
        for kc in range(2):
            for hf in range(2):
                conv_weight(Wuq[:, kc, hf * 384:(hf + 1) * 384], w_uq_d[l, kc * 128:(kc + 1) * 128, hf * 384:(hf + 1) * 384], 384,
                            vecs[:, l, 11 + kc:12 + kc], Wuq.k, "pool")
        for hf in range(2):
            conv_weight(Wukv[:, hf * 512:(hf + 1) * 512], w_ukv_d[l, :, hf * 512:(hf + 1) * 512], 512, vecs[:, l, 13:14], Wukv.k, "pool")
        conv_weight(Wup[0:16, :], w_up_d[l, :, :], 256, None, Wup.k, "pool")
        sc.op("pool", lambda e: e.tensor_scalar(out=negb[:], in0=vecs[:, l, 8:10], scalar1=-1.0, scalar2=1.0,
                                                op0=MUL, op1=MUL), r=[vecs.k], w=[negb.k])
        sc.dma(qg[:], qg_d[l, :].partition_broadcast(128), w=[qg.k])
        sc.dma(kg[:], kg_d[l, :].partition_broadcast(128), w=[kg.k])

    def prep_wukt(l):
        for h in range(4):
            sc.op("pe", lambda e, h=h: e.transpose(ptr[:, h * 128:(h + 1) * 128], Wukv[:, h * 256:h * 256 + 128], ident[:]),
                  r=[Wukv.k, ident.k], w=[ptr.k], signal=(h == 3))
        sc.op("dve", lambda e: e.tensor_scalar(out=WukT[:], in0=ptr[:, 0:512].rearrange("p (h c) -> p h c", h=4),
                                               scalar1=vecs[:, l, 14:15], scalar2=None, op0=MUL),
              r=[ptr.k, vecs.k], w=[WukT.k])

    def prep_wout(l):
        i = 0
        for kc in range(8):
            sap = vecs[:, l, 10:11] if kc < 4 else None
            for hf in range(2):
                conv_weight(Wout[:, kc, hf * 512:(hf + 1) * 512], w_out_d[l, kc * 128:(kc + 1) * 128, hf * 512:(hf + 1) * 512], 512,
                            sap, Wout.k, ("pool", "dve")[i % 2])
                i += 1

    def rstd_from_ss(st, c0, c1, n_elems, extra_bias_ln=None):
        sc.op("act", lambda e: e.activation(out=st[:, c0:c1], in_=st[:, c0:c1], func=AF.Ln, scale=1.0 / n_elems, bias=EPS),
              r=[st.k], w=[st.k])
        if extra_bias_ln is None:
            sc.op("act", lambda e: e.activation(out=st[:, c0:c1], in_=st[:, c0:c1], func=AF.Exp, scale=-0.5),
                  r=[st.k], w=[st.k])
        else:
            sc.op("act", lambda e: e.activation(out=st[:, c0:c1], in_=st[:, c0:c1], func=AF.Exp, scale=-0.5,
                                                bias=extra_bias_ln), r=[st.k], w=[st.k])

    def transposes(srcs, dst_views, n_rows_list):
        pass

    class V_:
        def __init__(self, ap, tok):
            self.t, self.k = ap, tok

        def __getitem__(self, idx):
            return self.t[idx]

    G0, G1 = pst[0], pst[1]
    GT = V_(pO[0].t[:].rearrange("p a b -> p (a b)").bitcast(BF16), pO[0].k)
    PN = V_(pO[1].t[:].rearrange("p a b -> p (a b)").bitcast(BF16), pO[1].k)
    junkG = sb("junkG", [128, 256], BF16)
    Ocp = sb("Ocp", [128, 4, 129], F32)

    def stage_load_x(l, t, buf):
        sc.dma(buf[:], xbufs[l][t * 128:(t + 1) * 128, :], r=[xk(l, t)], w=[buf.k])

    def gen_norm(l, t):
        xb_ = xt[0]
        h_T = hT[t % 2]
        sc.op("act", lambda e: e.activation(out=hb[:], in_=xb_[:], func=AF.Square, accum_out=st_x[:, 0:1]),
              r=[xb_.k], w=[hb.k, st_x.k])
        yield SLP["n"]
        rstd_from_ss(st_x, 0, 1, D)
        yield SLP["n"]
        sc.op("dve", lambda e: e.tensor_scalar(out=hb[:], in0=xb_[:], scalar1=st_x[:, 0:1], scalar2=None, op0=MUL),
              r=[xb_.k, st_x.k], w=[hb.k])
        if t + 1 < NT:
            stage_load_x(l, t + 1, xt[0])
        yield SLP["n"]
        for kc in range(8):
            sc.op("pe", lambda e, kc=kc: e.transpose(PN[:, kc * 128:(kc + 1) * 128], hb[:, kc * 128:(kc + 1) * 128], ident[:]),
                  r=[hb.k, ident.k], w=[PN.k], signal=(kc == 7))
        yield SLP["n"]
        sc.op("dve", lambda e: e.tensor_copy(out=h_T[:], in_=PN[:, :].rearrange("p (k c) -> p k c", k=8)),
              r=[PN.k], w=[h_T.k])
        yield SLP["n"]
        PNf = pO[1].t[:].rearrange("p a b -> p (a b)")
        for kc in range(8):
            sc.op("pe", lambda e, kc=kc: e.matmul(PNf[:, 0:128], lhsT=h_T[:, kc, :], rhs=Win[:, kc, C_LR:C_LR + 128],
                                                  start=(kc == 0), stop=(kc == 7)),
                  r=[h_T.k, Win.k], w=[PN.k], signal=(kc == 7))
        yield SLP["n"]
        sc.op("act", lambda e: e.copy(out=lrb[:], in_=PNf[:, 0:128]), r=[PN.k], w=[lrb.k])
        yield SLP["n"]
        sc.op("pe", lambda e: e.transpose(PN[:, 512:640], lrb[:], ident[:]), r=[lrb.k, ident.k], w=[PN.k], signal=True)
        yield SLP["n"]
        sc.op("dve", lambda e: e.tensor_copy(out=glrT[t % 2][:], in_=PN[:, 512:640]), r=[PN.k], w=[glrT[t % 2].k])
        yield SLP["n"]

    def inproj_tok(h_T, c0, n, out_ap, out_tok):
        for kc in range(8):
            sc.op("pe", lambda e, kc=kc: e.matmul(out_ap, lhsT=h_T[:, kc, :], rhs=Win[:, kc, c0:c0 + n],
                                                  start=(kc == 0), stop=(kc == 7)),
                  r=[h_T.k, Win.k], w=[out_tok], signal=(kc == 7))

    def gen_silu_gate(src_ps, c0, tmp, dst_ap, dst_tok):
        sc.op("act", lambda e: e.activation(out=tmp[:], in_=src_ps[:, c0:c0 + 512], func=AF.Exp, scale=-1.0),
              r=[src_ps.k], w=[tmp.k])
        yield SLP["s"]
        sc.op("act", lambda e: e.activation(out=tmp[:], in_=tmp[:], func=AF.Ln, bias=1.0), r=[tmp.k], w=[tmp.k])
        yield SLP["s"]
        sc.op("act", lambda e: e.activation(out=tmp[:], in_=tmp[:], func=AF.Exp, scale=-1.0), r=[tmp.k], w=[tmp.k])
        yield SLP["s"]
        sc.op("dve", lambda e: e.tensor_tensor(out=dst_ap, in0=src_ps[:, c0:c0 + 512], in1=tmp[:], op=MUL),
              r=[src_ps.k, tmp.k], w=[dst_tok])
        yield

    GTf = V_(pO[0].t[:].rearrange("p a b -> p (a b)"), pO[0].k)

    def gen_gla_gates(l, t):
        for c2 in range(2):
            sc.op("pe", lambda e, c2=c2: e.matmul(GTf[:, 256 + c2 * 128:256 + (c2 + 1) * 128],
                                                  lhsT=Wup[:, c2 * 128:(c2 + 1) * 128], rhs=glrT[t % 2][:], start=True, stop=True),
                  r=[Wup.k, glrT[t % 2].k], w=[GTf.k], signal=(c2 == 1))
        yield SLP["a"]
        for c2 in range(2):
            sc.op("act", lambda e, c2=c2: e.activation(out=e1[:, c2, :], in_=GTf[:, 256 + c2 * 128:256 + (c2 + 1) * 128],
                                                       func=AF.Exp, scale=-1.0, bias=negb[:, c2:c2 + 1]),
                  r=[GTf.k, negb.k], w=[e1.k])
        yield SLP["a"]
        sc.op("act", lambda e: e.activation(out=e1[:], in_=e1[:], func=AF.Ln, bias=1.0), r=[e1.k], w=[e1.k])
        yield SLP["a"]
        for c2 in range(2):
            sc.op("dve", lambda e, c2=c2: e.tensor_tensor_scan(out=csb[:, c2, :], data0=ones128[:], data1=e1[:, c2, :],
                                                               initial=0.0, op0=MUL, op1=ADD),
                  r=[ones128.k, e1.k], w=[csb.k])
        yield SLP["a"]
        sc.op("act", lambda e: e.activation(out=Epl[:], in_=csb[:], func=AF.Exp, scale=-1.0 / 16.0), r=[csb.k], w=[Epl.k])
        sc.op("act", lambda e: e.activation(out=Emi[:], in_=csb[:], func=AF.Exp, scale=1.0 / 16.0), r=[csb.k], w=[Emi.k])
        yield SLP["a"]

    PNf32 = V_(pO[1].t[:].rearrange("p a b -> p (a b)"), pO[1].k)

    def gen_gate_g(l, t):
        h_T = hT[t % 2]
        inproj_tok(h_T, C_GG, 512, PNf32[:, 0:512], PNf32.k)
        yield 1
        yield from gen_silu_gate(PNf32, 0, sg_e, sgate[:], sgate.k)

    def gen_gate_m(l, t):
        h_T = hT[t % 2]
        ti = t % 4
        inproj_tok(h_T, C_MG, 512, PNf32[:, 0:512], PNf32.k)
        yield 1
        yield from gen_silu_gate(PNf32, 0, mg_e, smg[:, ti, :], smg.k)

    def gen_gates(l, t):
        yield from gen_gate_g(l, t)
        yield from gen_gate_m(l, t)

    def gen_norm_then_gates(l, t):
        if VARIANT != "glast":
            yield from gen_gate_g(l, t)
            if t + 1 < NT:
                yield from gen_norm(l, t + 1)
            yield from gen_gate_m(l, t)
        else:
            if t + 1 < NT:
                yield from gen_norm(l, t + 1)
            yield from gen_gates(l, t)

    def gen_gla(l, t):
        h_T = hT[t % 2]
        tc0 = (t % 4) * 128
        for c in range(4):
            col = (C_GQ if c < 2 else C_GK) + (c % 2) * 128
            for kc in range(8):
                sc.op("pe", lambda e, kc=kc, c=c, col=col: e.matmul(G0[:, c * 128:(c + 1) * 128],
                                                                     lhsT=Win[:, kc, col:col + 128], rhs=h_T[:, kc, :],
                                                                     start=(kc == 0), stop=(kc == 7)),
                      r=[h_T.k, Win.k], w=[G0.k], signal=(kc == 7 and c == 3))
            yield SLP["g"]
        inproj_tok(h_T, C_GV, 512, G1[:, 0:512], G1.k)
        yield 1
        sc.op("act", lambda e: e.copy(out=vb[:], in_=G1[:, 0:512]), r=[G1.k], w=[vb.k])
        yield SLP["g"]
        for hh in range(2):
            sc.op("dve", lambda e, hh=hh: e.scalar_tensor_tensor(out=qdm[:, :, hh, :],
                                                                 in0=G0[:, 0:256].rearrange("p (c t) -> p c t", c=2),
                                                                 scalar=hmask[:, hh:hh + 1], in1=Epl[:], op0=MUL, op1=MUL),
                  r=[G0.k, Epl.k, hmask.k], w=[qdm.k])
        sc.op("dve", lambda e: e.tensor_tensor(out=kiT[:], in0=G0[:, 256:512].rearrange("p (c t) -> p c t", c=2),
                                               in1=Emi[:], op=MUL), r=[G0.k, Emi.k], w=[kiT.k])
        yield SLP["g"]
        for c2 in range(2):
            sc.op("pe", lambda e, c2=c2: e.transpose(GT[:, c2 * 128:(c2 + 1) * 128], kiT[:, c2, :], ident[:]),
                  r=[kiT.k, ident.k], w=[GT.k], signal=(c2 == 1))
        for h in range(4):
            c2, hh = h // 2, h % 2
            sc.op("pe", lambda e, h=h, c2=c2, hh=hh: e.matmul(G0[:, h * 128:(h + 1) * 128], lhsT=kiT[:, c2, :],
                                                              rhs=qdm[:, c2, hh, :], start=True, stop=True),
                  r=[kiT.k, qdm.k], w=[G0.k], signal=(h == 3))
        yield SLP["g"]
        sc.op("dve", lambda e: e.tensor_copy(out=ki[:], in_=GT[:, 0:256].rearrange("p (c t) -> p c t", c=2)), r=[GT.k], w=[ki.k])
        sc.op("pool", lambda e: e.tensor_copy(out=kiA[:, :, 0:64], in_=ki[:, :, 0:64]), r=[ki.k], w=[kiA.k])
        sc.op("pool", lambda e: e.tensor_copy(out=kiB[:, :, 64:128], in_=ki[:, :, 64:128]), r=[ki.k], w=[kiB.k])
        sc.op("dve", lambda e: e.tensor_tensor(out=Am[:], in0=G0[:, 0:512].rearrange("p (h t) -> p h t", h=4),
                                               in1=triu4[:], op=MUL), r=[G0.k, triu4.k], w=[Am.k])
        yield SLP["g"]
        for h in range(4):
            c2, hh = h // 2, h % 2
            sc.op("pe", lambda e, h=h: e.matmul(G1[:, h * 128:(h + 1) * 128], lhsT=Am[:, h, :],
                                                rhs=vb[:, h * 128:(h + 1) * 128], start=True, stop=False),
                  r=[Am.k, vb.k], w=[G1.k], signal=False)
            sc.op("pe", lambda e, h=h, c2=c2, hh=hh: e.matmul(G1[:, h * 128:(h + 1) * 128], lhsT=qdm[:, c2, hh, :],
                                                              rhs=Sbf[:, c2, :], start=False, stop=True),
                  r=[qdm.k, Sbf.k], w=[G1.k], signal=(h == 3))
        for c2 in range(2):
            sc.op("pe", lambda e, c2=c2: e.matmul(G0[:, c2 * 128:(c2 + 1) * 128], lhsT=kiA[:, c2, :],
                                                  rhs=vb[:, (2 * c2) * 128:(2 * c2 + 1) * 128], start=True, stop=False),
                  r=[kiA.k, vb.k], w=[G0.k], signal=False)
            sc.op("pe", lambda e, c2=c2: e.matmul(G0[:, c2 * 128:(c2 + 1) * 128], lhsT=kiB[:, c2, :],
                                                  rhs=vb[:, (2 * c2 + 1) * 128:(2 * c2 + 2) * 128], start=False, stop=True),
                  r=[kiB.k, vb.k], w=[G0.k], signal=(c2 == 1))
        yield SLP["g"]
        for h in range(4):
            sc.op("act", lambda e, h=h: e.activation(out=junkG[:, 0:128], in_=G1[:, h * 128:(h + 1) * 128], func=AF.Square,
                                                     accum_out=st_g[:, h:h + 1]), r=[G1.k], w=[junkG.k, st_g.k])
        yield SLP["g"]
        sc.op("dve", lambda e: e.tensor_tensor(out=Sd[:], in0=G0[:, 0:256].rearrange("p (c t) -> p c t", c=2), in1=Sst[:], op=ADD),
              r=[G0.k, Sst.k], w=[Sd.k])
        yield SLP["g"]
        for c2 in range(2):
            sc.op("pool", lambda e, c2=c2: e.tensor_scalar(out=Sst[:, c2, :], in0=Sd[:, c2, :], scalar1=Epl[:, c2, 127:128],
                                                           scalar2=1.0, op0=MUL, op1=MUL), r=[Sd.k, Epl.k], w=[Sst.k])
        sc.op("pool", lambda e: e.tensor_copy(out=Sbf[:], in_=Sst[:]), r=[Sst.k], w=[Sbf.k])
        yield SLP["g"]
        rstd_from_ss(st_g, 0, 4, 128)
        yield SLP["g"]
        for h in range(4):
            sc.op("dve", lambda e, h=h: e.scalar_tensor_tensor(out=ogl[:, h, :], in0=G1[:, h * 128:(h + 1) * 128],
                                                               scalar=st_g[:, h:h + 1], in1=sgate[:, h * 128:(h + 1) * 128],
                                                               op0=MUL, op1=MUL), r=[G1.k, st_g.k, sgate.k], w=[ogl.k])
        yield SLP["g"]
        for h in range(4):
            sc.op("pe", lambda e, h=h: e.transpose(GT[:, h * 128:(h + 1) * 128], ogl[:, h, :], ident[:]),
                  r=[ogl.k, ident.k], w=[GT.k], signal=(h == 3))
        yield SLP["g"]
        sc.op("dve", lambda e: e.tensor_copy(out=ogT[:, 0:4, tc0:tc0 + 128],
                                             in_=GT[:, 0:512].rearrange("p (h c) -> p h c", h=4)),
              r=[GT.k], w=[ogT.k])
        yield SLP["g"]

    def gen_mla(l, t):
        h_T = hT[t % 2]
        ti = t % 4
        tc0 = ti * 128
        inproj_tok(h_T, C_CQ, 448, pmm[:, 0:448], pmm.k)
        yield 1
        sc.op("act", lambda e: e.activation(out=junk[:, 0:256], in_=pmm[:, 0:256], func=AF.Square, accum_out=st_c[:, 0:1]),
              r=[pmm.k], w=[junk.k, st_c.k])
        yield SLP["m"]
        sc.op("act", lambda e: e.activation(out=junk[:, 0:128], in_=pmm[:, 256:384], func=AF.Square, accum_out=st_c[:, 1:2]),
              r=[pmm.k], w=[junk.k, st_c.k])
        yield SLP["m"]
        sc.op("act", lambda e: e.activation(out=junk[:, 0:64], in_=pmm[:, 384:448], func=AF.Square, accum_out=st_c[:, 2:3]),
              r=[pmm.k], w=[junk.k, st_c.k])
        yield SLP["m"]
        rstd_from_ss(st_c, 0, 1, 256)
        yield SLP["m"]
        rstd_from_ss(st_c, 1, 2, 128)
        yield SLP["m"]
        sc.op("dve", lambda e: e.tensor_scalar(out=cn[:, 0:256], in0=pmm[:, 0:256], scalar1=st_c[:, 0:1], scalar2=None, op0=MUL),
              r=[pmm.k, st_c.k], w=[cn.k])
        sc.op("dve", lambda e: e.tensor_scalar(out=cn[:, 256:384], in0=pmm[:, 256:384], scalar1=st_c[:, 1:2], scalar2=None, op0=MUL),
              r=[pmm.k, st_c.k], w=[cn.k])
        sc.op("dve", lambda e: e.tensor_tensor(out=kpg[:], in0=pmm[:, 384:448], in1=kg[:, 128:192], op=MUL),
              r=[pmm.k, kg.k], w=[kpg.k])
        yield SLP["m"]
        for c in range(3):
            sc.op("pe", lambda e, c=c: e.transpose(ptr[:, c * 128:(c + 1) * 128], cn[:, c * 128:(c + 1) * 128], ident[:]),
                  r=[cn.k, ident.k], w=[ptr.k], signal=(c == 2))
        yield SLP["m"]
        sc.op("dve", lambda e: e.tensor_copy(out=cnT[:], in_=ptr[:, 0:384].rearrange("p (c t) -> p c t", c=3)),
              r=[ptr.k], w=[cnT.k])
        sc.op("pool", lambda e: e.tensor_copy(out=KT[:, t * 128:(t + 1) * 128], in_=cnT[:, 2, :]), r=[cnT.k], w=[KT.k])
        yield SLP["m"]
        c1_, s1_ = cosT[:, t, :], sinT[:, t, :]
        a1, a2 = kpg[:, 0:32], kpg[:, 32:64]
        r0_, r1_, r2_, r3_ = (rtmp[i][:, 0, :] for i in range(4))
        sc.op("pool", lambda e: e.tensor_tensor(out=r0_, in0=a1, in1=c1_, op=MUL), r=[kpg.k, cosT.k], w=[rtmp[0].k])
        sc.op("pool", lambda e: e.tensor_tensor(out=r1_, in0=a2, in1=s1_, op=MUL), r=[kpg.k, sinT.k], w=[rtmp[1].k])
        sc.op("pool", lambda e: e.tensor_tensor(out=krb[:, 0:32], in0=r0_, in1=r1_, op=SUB), r=[rtmp[0].k, rtmp[1].k], w=[krb.k])
        sc.op("pool", lambda e: e.tensor_tensor(out=r2_, in0=a1, in1=s1_, op=MUL), r=[kpg.k, sinT.k], w=[rtmp[2].k])
        sc.op("pool", lambda e: e.tensor_tensor(out=r3_, in0=a2, in1=c1_, op=MUL), r=[kpg.k, cosT.k], w=[rtmp[3].k])
        sc.op("pool", lambda e: e.tensor_tensor(out=krb[:, 32:64], in0=r2_, in1=r3_, op=ADD), r=[rtmp[2].k, rtmp[3].k], w=[krb.k])
        for (o0, n) in ((0, 512), (512, 256)):
            for kc in range(2):
                sc.op("pe", lambda e, kc=kc, o0=o0, n=n: e.matmul(pbig[:, o0:o0 + n], lhsT=cnT[:, kc, :],
                                                                  rhs=Wuq[:, kc, o0:o0 + n], start=(kc == 0), stop=(kc == 1)),
                      r=[cnT.k, Wuq.k], w=[pbig.k], signal=(kc == 1 and o0 == 512))
        yield 1
        for h in range(4):
            sc.op("act", lambda e, h=h: e.activation(out=junk[:, 0:192], in_=pbig[:, h * 192:(h + 1) * 192], func=AF.Square,
                                                     accum_out=st_q[:, h:h + 1]), r=[pbig.k], w=[junk.k, st_q.k])
            yield SLP["m"]
        rstd_from_ss(st_q, 0, 4, 192)
        yield SLP["m"]
        for h in range(4):
            sc.op("dve", lambda e, h=h: e.scalar_tensor_tensor(out=qn[:, h, :], in0=pbig[:, h * 192:(h + 1) * 192],
                                                               scalar=st_q[:, h:h + 1], in1=qg[:], op0=MUL, op1=MUL),
                  r=[pbig.k, st_q.k, qg.k], w=[qn.k])
            yield SLP["m"]
        for half in range(2):
            sc.op("pe", lambda e, half=half: e.matmul(pbig[:, half * 512:(half + 1) * 512], lhsT=cnT[:, 2, :],
                                                      rhs=Wukv[:, half * 512:(half + 1) * 512], start=True, stop=True),
                  r=[cnT.k, Wukv.k], w=[pbig.k], signal=(half == 1))
        sc.op("pe", lambda e: e.transpose(ptr[:, 512:640], krb[:], ident[:]), r=[krb.k, ident.k], w=[ptr.k], signal=True)
        yield SLP["m"]
        sc.op("dve", lambda e: e.tensor_copy(out=KTr[:, t * 128:(t + 1) * 128], in_=ptr[:, 512:640]), r=[ptr.k], w=[KTr.k])
        sc.op("pool", lambda e: e.tensor_copy(out=qb[:, :, 0:128], in_=qn[:, :, 0:128]), r=[qn.k], w=[qb.k])
        cos4 = cosT[:, t, :].unsqueeze(1).broadcast_to([128, 4, 32])
        sin4 = sinT[:, t, :].unsqueeze(1).broadcast_to([128, 4, 32])
        t1, t2 = qn[:, :, 128:160], qn[:, :, 160:192]
        sc.op("pool", lambda e: e.tensor_tensor(out=rtmp[0][:], in0=t1, in1=cos4, op=MUL), r=[qn.k, cosT.k], w=[rtmp[0].k])
        sc.op("pool", lambda e: e.tensor_tensor(out=rtmp[1][:], in0=t2, in1=sin4, op=MUL), r=[qn.k, sinT.k], w=[rtmp[1].k])
        yield SLP["m"]
        sc.op("pool", lambda e: e.tensor_tensor(out=qb[:, :, 128:160], in0=rtmp[0][:], in1=rtmp[1][:], op=SUB),
              r=[rtmp[0].k, rtmp[1].k], w=[qb.k])
        sc.op("pool", lambda e: e.tensor_tensor(out=rtmp[2][:], in0=t1, in1=sin4, op=MUL), r=[qn.k, sinT.k], w=[rtmp[2].k])
        yield SLP["m"]
        sc.op("pool", lambda e: e.tensor_tensor(out=rtmp[3][:], in0=t2, in1=cos4, op=MUL), r=[qn.k, cosT.k], w=[rtmp[3].k])
        sc.op("pool", lambda e: e.tensor_tensor(out=qb[:, :, 160:192], in0=rtmp[2][:], in1=rtmp[3][:], op=ADD),
              r=[rtmp[2].k, rtmp[3].k], w=[qb.k])
        yield SLP["m"]
        kv4 = pbig[:, :].rearrange("p (h c) -> p h c", h=4)
        for h in range(4):
            sc.op("act", lambda e, h=h: e.activation(out=junk[:, 0:128], in_=pbig[:, h * 256:h * 256 + 128], func=AF.Square,
                                                     accum_out=st_k[:, h:h + 1]), r=[pbig.k], w=[junk.k, st_k.k])
            yield SLP["m"]
        sc.op("act", lambda e: e.copy(out=Vc[:, t, :, 0:128], in_=kv4[:, :, 128:256]), r=[pbig.k], w=[Vc.k])
        sc.op("dve", lambda e: e.tensor_scalar(out=st_k[:, 0:4], in0=st_k[:, 0:4], scalar1=st_c[:, 2:3], scalar2=None, op0=ADD),
              r=[st_k.k, st_c.k], w=[st_k.k])
        yield SLP["m"]
        rstd_from_ss(st_k, 0, 4, 192, extra_bias_ln=float(np.log(192.0 ** -0.5)))
        yield SLP["m"]
        sc.op("pool", lambda e: e.tensor_copy(out=rks[:, t, :], in_=st_k[:, 0:4]), r=[st_k.k], w=[rks.k])
        for h in range(4):
            sc.op("pe", lambda e, h=h: e.transpose(ptr[:, h * 128:(h + 1) * 128], qb[:, h, 0:128], ident[:]),
                  r=[qb.k, ident.k], w=[ptr.k], signal=False)
        for h in range(4):
            sc.op("pe", lambda e, h=h: e.transpose(ptr[:, 512 + h * 128:512 + (h + 1) * 128], qb[:, h, 128:256], ident[:]),
                  r=[qb.k, ident.k], w=[ptr.k], signal=(h == 3))
        yield SLP["m"]
        sc.op("dve", lambda e: e.tensor_copy(out=QTn[:, :, tc0:tc0 + 128], in_=ptr[:, 0:512].rearrange("p (h c) -> p h c", h=4)),
              r=[ptr.k], w=[QTn.k])
        sc.op("dve", lambda e: e.tensor_copy(out=QTr[:, :, tc0:tc0 + 128],
                                             in_=ptr[:, 512:1024].rearrange("p (h c) -> p h c", h=4)),
              r=[ptr.k], w=[QTr.k])
        yield SLP["m"]

    step_ctr = [0]

    def run_interleaved(gens):
        gens = [g for g in gens if g is not None]
        sleep = {}
        lim = int(VARIANT.split(":")[1]) if VARIANT.startswith("steps:") else None
        while gens:
            progressed = False
            for g in list(gens):
                if sleep.get(id(g), 0) > 0 and len(gens) > 1:
                    sleep[id(g)] -= 1
                    continue
                if lim is not None and step_ctr[0] >= lim:
                    raise _Stop()
                step_ctr[0] += 1
                progressed = True
                try:
                    res = next(g)
                    if isinstance(res, tuple) and res[0] == "spawn":
                        gens.append(res[1])
                    elif isinstance(res, int) and PIPE_DIST:
                        sleep[id(g)] = res * PIPE_DIST
                except StopIteration:
                    while g in gens:
                        gens.remove(g)
            if not progressed:
                for k in sleep:
                    sleep[k] = 0

    pt_i = [0]
    ps_i = [0]

    def stage_attention(l, b, bg=None):
        outs = [(pst[0], pst[0][:, 0:512]), (pst[1], pst[1][:, 0:512]), (pmm, pmm[:, 0:512]), (pbig, pbig[:, 0:512])]
        for h in range(4):
            tk, ap = outs[h]
            sc.op("pe", lambda e, h=h, ap=ap: e.matmul(ap, lhsT=WukT[:, h, :], rhs=QTn[:, h, :], start=True, stop=True),
                  r=[WukT.k, QTn.k], w=[tk.k], signal=True)
        for h in range(4):
            tk, ap = outs[h]
            sc.op("dve", lambda e, h=h, ap=ap: e.tensor_copy(out=QTn[:, h, :], in_=ap), r=[tk.k], w=[QTn.k])
        nk = 4 * b + 4
        steps = [(h, j) for h in range(4) for j in range(nk)]
        bufs = {}

        def emit_st(i):
            h, j = steps[i]
            r = j - 4 * b
            q0 = 128 * r if r > 0 else 0
            n = 512 - q0
            pst_ = pst3[ps_i[0] % 3]
            ps_i[0] += 1
            bufs[i] = pst_
            sc.op("pe", lambda e: e.matmul(pst_[:, 0:n], lhsT=KT[:, j * 128:(j + 1) * 128], rhs=QTn[:, h, q0:512],
                                           start=True, stop=False), r=[KT.k, QTn.k], w=[pst_.k], signal=False)
            sc.op("pe", lambda e: e.matmul(pst_[:, 0:n], lhsT=KTr[:, j * 128:(j + 1) * 128], rhs=QTr[:, h, q0:512],
                                           start=False, stop=True), r=[KTr.k, QTr.k], w=[pst_.k], signal=True)

        first = [True, True]
        pst3 = [pst[0], pst[1], pmm]
        emit_st(0)
        if len(steps) > 1:
            emit_st(1)
        for i, (h, j) in enumerate(steps):
            if bg is not None and i % 2 == 1:
                next(bg, None)
            if i + 2 < len(steps):
                emit_st(i + 2)
            r = j - 4 * b
            q0 = 128 * r if r > 0 else 0
            n = 512 - q0
            pst_ = bufs.pop(i)
            PT_ = PT[pt_i[0] % 4]
            pt_i[0] += 1
            sc.op("act", lambda e: e.activation(out=PT_[:, 0:n], in_=pst_[:, 0:n], func=AF.Exp, scale=rks[:, j, h:h + 1]),
                  r=[pst_.k, rks.k], w=[PT_.k])
            if r >= 0:
                sc.op("pool", lambda e: e.tensor_tensor(out=PT_[:, 0:128], in0=PT_[:, 0:128], in1=triu4[:, 0, :], op=MUL),
                      r=[PT_.k, triu4.k], w=[PT_.k])
            if j == 0:
                first = [True, True]
            for s in range(max(r, 0), 4):
                bank = s // 2
                st_flag = first[bank]
                first[bank] = False
                last = (j == 4 * b + s)
                c0 = 128 * s - q0
                sc.op("pe", lambda e, s=s, bank=bank, st_flag=st_flag, last=last, c0=c0: e.matmul(
                    pO[bank][:, s % 2, 0:129], lhsT=PT_[:, c0:c0 + 128], rhs=Vc[:, j, h, :],
                    start=st_flag, stop=last, skip_group_check=True),
                    r=[PT_.k, Vc.k], w=[pO[bank].k], signal=(s == 3))
            if j == nk - 1:
                for bank in range(2):
                    sc.op("dve", lambda e, bank=bank: e.tensor_copy(out=Ocp[:, 2 * bank:2 * bank + 2, :],
                                                                    in_=pO[bank][:, :, 0:129]),
                          r=[pO[bank].k], w=[Ocp.k])
                sc.op("dve", lambda e: e.reciprocal(out=rinv[:, 0:4], in_=Ocp[:, :, 128]), r=[Ocp.k], w=[rinv.k])
                for s in range(4):
                    sc.op("dve", lambda e, s=s: e.scalar_tensor_tensor(out=om[:, s, :], in0=Ocp[:, s, 0:128],
                                                                       scalar=rinv[:, s:s + 1],
                                                                       in1=smg[:, s, h * 128:(h + 1) * 128], op0=MUL, op1=MUL),
                          r=[Ocp.k, rinv.k, smg.k], w=[om.k])
                for s in range(4):
                    sc.op("pe", lambda e, s=s: e.transpose(ptr[:, s * 128:(s + 1) * 128], om[:, s, :], ident[:]),
                          r=[om.k, ident.k], w=[ptr.k], signal=(s == 3))
                sc.op("dve", lambda e: e.tensor_copy(out=ogT[:, 4 + h, :], in_=ptr[:, 0:512]), r=[ptr.k], w=[ogT.k])

    def xr_load(l, t):
        xr_ = xr[t % 2]
        sc.dma(xr_[:], xbufs[l][t * 128:(t + 1) * 128, :], r=[xk(l, t)], w=[xr_.k])

    def stage_outproj(l, b):
        for s in range(4):
            t = 4 * b + s
            xr_ = xr[t % 2]
            for half in range(2):
                for kc in range(8):
                    sc.op("pe", lambda e, half=half, kc=kc: e.matmul(pbig[:, half * 512:(half + 1) * 512],
                                                                     lhsT=ogT[:, kc, s * 128:(s + 1) * 128],
                                                                     rhs=Wout[:, kc, half * 512:(half + 1) * 512],
                                                                     start=(kc == 0), stop=(kc == 7)),
                          r=[ogT.k, Wout.k], w=[pbig.k], signal=(kc == 7 and half == 1))
            sc.op("dve", lambda e: e.tensor_tensor(out=xr_[:], in0=pbig[:, :], in1=xr_[:], op=ADD),
                  r=[pbig.k, xr_.k], w=[xr_.k])
            sc.dma(xbufs[l + 1][t * 128:(t + 1) * 128, :], xr_[:], r=[xr_.k], w=[xk(l + 1, t)])
            if s + 2 < 4:
                xr_load(l, t + 2)

    def chk(name):
        if stop_after == name:
            raise _Stop()

    try:
        pending_wout = [None]
        chk("init")
        prep_win(0)
        prep_small(0)
        prep_wukt(0)
        prep_wout(0)
        chk("prep")
        for l in range(depth):
            sc.op("pool", lambda e: e.memset(Sst[:], 0.0), w=[Sst.k])
            sc.op("pool", lambda e: e.memset(Sbf[:], 0.0), w=[Sbf.k])
            stage_load_x(l, 0, xt[0])
            run_interleaved([gen_norm(l, 0)])
            chk("norm0")
            for t in range(NT):
                gm = gen_mla(l, t)
                extra = pending_wout[0]
                pending_wout[0] = None
                if VARIANT == "mla2":
                    run_interleaved([gm, gen_gla_gates(l, t), gen_gla(l, t), gm, gen_norm_then_gates(l, t)])
                else:
                    chains = {"a": gen_gla_gates(l, t), "g": gen_gla(l, t), "m": gm, "n": gen_norm_then_gates(l, t)}
                    order = VARIANT[4:] if VARIANT.startswith("ord:") else "agmn"
                    run_interleaved([chains[c] for c in order] + [extra])
                chk("mla")
                bg = None
                if t == NT - 1 and l + 1 < depth:
                    bg = gen_prep(l + 1, ("win", "small"))
                if t % 4 == 3:
                    b = t // 4
                    xr_load(l, 4 * b)
                    xr_load(l, 4 * b + 1)
                    if VARIANT != "noattn":
                        stage_attention(l, b, bg)
                    if bg is not None:
                        for _ in bg:
                            pass
                    chk("attn")
                    stage_outproj(l, b)
                    chk("out")
            if l + 1 < depth:
                prep_wukt(l + 1)
                pending_wout[0] = gen_prep(l + 1, ("wout",))
    except _Stop:
        pass
    for en in ("pe", "act", "dve", "pool"):
        e_ = sc.eng[en]
        if e_["n"] > 0:
            sc._wait("sp", Ev(en, e_["sem"], e_["n"]))
    sc.wait_all_dma("sp")
    nc._sched_ninst = sc.ninst
    return nc


def _host_inputs(inputs, S, depth):
    f32 = np.float32
    vecs = np.zeros((depth, 128, NVEC), f32)
    vecs[:, :, 0:8] = inputs["norm_g"][:depth].reshape(depth, 8, 128).transpose(0, 2, 1)
    vecs[:, :, 8:10] = inputs["b_gla_gate"][:depth].reshape(depth, 2, 128).transpose(0, 2, 1)
    vecs[:, :, 10] = inputs["gla_norm_g"][:depth]
    vecs[:, :, 11:13] = inputs["mla_q_norm_g"][:depth].reshape(depth, 2, 128).transpose(0, 2, 1)
    vecs[:, :, 13] = inputs["mla_kv_norm_g"][:depth]
    vecs[:, :, 14] = inputs["k_head_g"][:depth, 0:128]
    invf = (10000.0 ** (-np.arange(0, 64, 2, dtype=f32) / f32(64))).astype(f32)
    common = {
        "invf": np.ascontiguousarray(np.broadcast_to(invf[None, :], (128, 32))).astype(f32),
        "ident": np.eye(128, dtype=f32),
        "triu": np.triu(np.ones((128, 128), f32)),
        "w_in": np.ascontiguousarray(np.concatenate([inputs["w_in"][:depth, :, 0:1024], inputs["w_in"][:depth, :, 1040:1552],
                                                     inputs["w_in"][:depth, :, 1024:1040], inputs["w_in"][:depth, :, 1552:]], axis=2), dtype=f32),
        "w_up": np.ascontiguousarray(inputs["w_gla_gate_up"][:depth], dtype=f32),
        "vecs": vecs,
        "w_uq": np.ascontiguousarray(inputs["w_uq"][:depth], dtype=f32),
        "w_ukv": np.ascontiguousarray(inputs["w_ukv"][:depth], dtype=f32),
        "q_head_g": np.ascontiguousarray(inputs["q_head_g"][:depth], dtype=f32),
        "k_head_g": np.ascontiguousarray(inputs["k_head_g"][:depth], dtype=f32),
        "w_out": np.ascontiguousarray(inputs["w_out"][:depth], dtype=f32),
    }
    NT = S // 128
    in_maps = []
    for c in range(8):
        b = c % BATCH
        m = dict(common)
        m["x"] = np.ascontiguousarray(inputs["x"][b, :S], dtype=f32)
        m["pos"] = np.ascontiguousarray(inputs["positions"][b, :S].reshape(NT, 128).T.astype(np.int32))
        in_maps.append(m)
    return in_maps


_NC_CACHE = {}


def run(inputs, S=SEQ, depth=DEPTH):
    key = (S, depth)
    if key not in _NC_CACHE:
        _NC_CACHE[key] = build_program(S, depth)
    nc = _NC_CACHE[key]
    in_maps = _host_inputs(inputs, S, depth)
    res = run_bass_kernel_spmd(nc, in_maps, core_ids=list(range(8)))
    return np.stack([res.results[b]["y"] for b in range(BATCH)], axis=0)


def kernel(**inputs):
    inputs = {k: np.asarray(v) for k, v in inputs.items()}
    return run(inputs).astype(np.float32)
```
